# Optimizing a Trainium2 kernel written in Bass

```python
import math
import jax, jax.numpy as jnp
from jax import lax
import numpy as np

D_MODEL = 1024
BATCH = 2
SEQ = 16384
DEPTH = 2
DEC_BATCH = 4
DEC_SEQ = 4096
PAST_LEN = 128

HEAD_DIM = 64
N_HEADS = D_MODEL // HEAD_DIM
N_KV_HEADS = N_HEADS // 4
GROUP = N_HEADS // N_KV_HEADS
ATTN_WIDTH = N_HEADS * HEAD_DIM
KV_WIDTH = N_KV_HEADS * HEAD_DIM
ATTN_IN_WIDTH = 2 * ATTN_WIDTH + 2 * KV_WIDTH
WINDOW = 128
BLOCK = 128
NUM_BUCKETS = 32
MAX_DISTANCE = 128
N_FGROUPS = 4
FGROUP_CH = D_MODEL // N_FGROUPS
FOURIER_WIDTH = D_MODEL
RMS_EPS = 1e-6
N_MIXERS = 2
N_ATTN_LAYERS = (DEPTH + 1) // 2
N_FOURIER_LAYERS = DEPTH // 2

kernel_name = "hybrid_window_gqa_fnet_encoder"


def rmsnorm(x, g):
    xf = x.astype(jnp.float32)
    inv = lax.rsqrt(jnp.mean(xf * xf, axis=-1, keepdims=True) + RMS_EPS)
    return (xf * inv * g.astype(jnp.float32)).astype(x.dtype)


def t5_bucket_np(rel):
    half = NUM_BUCKETS // 2
    n = -rel
    ret = (n < 0).astype(np.int32) * half
    n = np.abs(n)
    max_exact = half // 2
    is_small = n < max_exact
    large = max_exact + (np.log(np.maximum(n, 1) / max_exact) / math.log(MAX_DISTANCE / max_exact)
                         * (half - max_exact)).astype(np.int32)
    large = np.minimum(large, half - 1)
    return (ret + np.where(is_small, n, large)).astype(np.int32)


def band_windows(t, nb):
    b = t.shape[0]
    tp = jnp.pad(t, ((0, 0), (BLOCK, BLOCK), (0, 0), (0, 0)))
    tb = tp.reshape(b, nb + 2, BLOCK, t.shape[2], t.shape[3])
    return jnp.concatenate([tb[:, :-2], tb[:, 1:-1], tb[:, 2:]], axis=2)


def banded_gqa_with_sink(q, k, v, rel_bias, sink):
    b, s = q.shape[0], q.shape[1]
    nb = s // BLOCK
    qb = q.reshape(b, nb, BLOCK, N_KV_HEADS, GROUP, HEAD_DIM)
    kw = band_windows(k, nb)
    vw = band_windows(v, nb)
    qi = np.arange(BLOCK)[:, None]
    kj = np.arange(3 * BLOCK)[None, :]
    rel = kj - BLOCK - qi
    band = np.abs(rel) <= WINDOW
    kpos = (np.arange(nb)[:, None] - 1) * BLOCK + np.arange(3 * BLOCK)[None, :]
    valid = (kpos >= 0) & (kpos < s)
    mask = jnp.asarray(band[None, :, :] & valid[:, None, :])
    bucket = jnp.asarray(t5_bucket_np(rel))
    bias = rel_bias.astype(jnp.float32)[bucket]
    bias = jnp.transpose(bias, (2, 0, 1)).reshape(N_KV_HEADS, GROUP, BLOCK, 3 * BLOCK)
    scale = HEAD_DIM ** -0.5
    scores = jnp.einsum('bnqhgd,bnkhd->bnhgqk', qb, kw).astype(jnp.float32) * scale + bias
    scores = jnp.where(mask[None, :, None, None, :, :], scores, -jnp.inf)
    sk = sink.astype(jnp.float32).reshape(N_KV_HEADS, GROUP)[None, None, :, :, None, None]
    m = jnp.maximum(jnp.max(scores, axis=-1, keepdims=True), sk)
    p = jnp.exp(scores - m)
    denom = jnp.sum(p, axis=-1, keepdims=True) + jnp.exp(sk - m)
    p = p / denom
    o = jnp.einsum('bnhgqk,bnkhd->bnqhgd', p, vw.astype(jnp.float32))
    return o.reshape(b, s, ATTN_WIDTH).astype(q.dtype)


def attention_layer(x, g, w_in, w_out, sink, rel_bias):
    b, s, _ = x.shape
    h = rmsnorm(x, g)
    z = h @ w_in
    q = z[..., :ATTN_WIDTH].reshape(b, s, N_HEADS, HEAD_DIM)
    k = z[..., ATTN_WIDTH:ATTN_WIDTH + KV_WIDTH].reshape(b, s, N_KV_HEADS, HEAD_DIM)
    v = z[..., ATTN_WIDTH + KV_WIDTH:ATTN_WIDTH + 2 * KV_WIDTH].reshape(b, s, N_KV_HEADS, HEAD_DIM)
    gate = z[..., ATTN_WIDTH + 2 * KV_WIDTH:]
    o = banded_gqa_with_sink(q, k, v, rel_bias, sink)
    return x + (o * jax.nn.silu(gate)) @ w_out


def fourier_layer(x, g, w_gate, w_out):
    b, s, d = x.shape
    h = rmsnorm(x, g)
    hg = h.astype(jnp.float32).reshape(b, s, N_FGROUPS, FGROUP_CH)
    f = jnp.fft.fftn(hg, axes=(1, 3), norm="ortho").real
    f = f.reshape(b, s, d).astype(x.dtype)
    gate = h @ w_gate
    return x + (f * jax.nn.silu(gate)) @ w_out


def trunk(x, rel_bias, attn_norm, attn_w_in, attn_w_out, attn_sink,
          fourier_norm, fourier_w_gate, fourier_w_out, final_norm):
    for i in range(DEPTH):
        j = i // N_MIXERS
        if i % N_MIXERS == 0:
            x = attention_layer(x, attn_norm[j], attn_w_in[j], attn_w_out[j], attn_sink[j], rel_bias)
        else:
            x = fourier_layer(x, fourier_norm[j], fourier_w_gate[j], fourier_w_out[j])
    return rmsnorm(x, final_norm)


def setup_inputs(seed: int = 0) -> dict:
    key = jax.random.key(seed)
    ks = jax.random.split(key, 12)
    f32 = jnp.float32
    return {
        "x_prompt": jax.random.normal(ks[0], (BATCH, SEQ, D_MODEL), f32),
        "x_sample": jax.random.normal(ks[1], (DEC_BATCH, DEC_SEQ, D_MODEL), f32),
        "rel_bias": 0.5 * jax.random.normal(ks[2], (NUM_BUCKETS, N_HEADS), f32),
        "attn_norm": 1.0 + 0.02 * jax.random.normal(ks[3], (N_ATTN_LAYERS, D_MODEL), f32),
        "attn_w_in": jax.random.normal(ks[4], (N_ATTN_LAYERS, D_MODEL, ATTN_IN_WIDTH), f32) * D_MODEL ** -0.5,
        "attn_w_out": jax.random.normal(ks[5], (N_ATTN_LAYERS, ATTN_WIDTH, D_MODEL), f32) * ATTN_WIDTH ** -0.5,
        "attn_sink": 0.5 * jax.random.normal(ks[6], (N_ATTN_LAYERS, N_HEADS), f32),
        "fourier_norm": 1.0 + 0.02 * jax.random.normal(ks[7], (N_FOURIER_LAYERS, D_MODEL), f32),
        "fourier_w_gate": jax.random.normal(ks[8], (N_FOURIER_LAYERS, D_MODEL, FOURIER_WIDTH), f32) * D_MODEL ** -0.5,
        "fourier_w_out": jax.random.normal(ks[9], (N_FOURIER_LAYERS, FOURIER_WIDTH, D_MODEL), f32) * FOURIER_WIDTH ** -0.5,
        "final_norm": 1.0 + 0.02 * jax.random.normal(ks[10], (D_MODEL,), f32),
    }


def reference(x_prompt, x_sample, rel_bias, attn_norm, attn_w_in, attn_w_out, attn_sink,
              fourier_norm, fourier_w_gate, fourier_w_out, final_norm):
    y_prompt = trunk(x_prompt, rel_bias, attn_norm, attn_w_in, attn_w_out, attn_sink,
                     fourier_norm, fourier_w_gate, fourier_w_out, final_norm)
    y_sample = trunk(x_sample, rel_bias, attn_norm, attn_w_in, attn_w_out, attn_sink,
                     fourier_norm, fourier_w_gate, fourier_w_out, final_norm)
    return (y_prompt, y_sample)
```

```python
import math
import os
import contextlib
import numpy as np
import ml_dtypes
import concourse.bass as bass
import concourse.mybir as mybir
from concourse.bass_utils import run_bass_kernel_spmd

F32 = mybir.dt.float32
BF16 = mybir.dt.bfloat16
AF = mybir.ActivationFunctionType
ALU = mybir.AluOpType
D = 1024
EPS = 1e-6
NEG = -30000.0


class Res:
    def __init__(self):
        self.w = {}
        self.r = {}
        self.dsem = None
        self.dkey = None
        self.dcnt = 0


class Sched:
    EPOCH = 12000

    def __init__(self, nc, es):
        self.nc = nc
        self.es = es
        self.nsem = 0
        self.eng = {}
        self.dma_res = []
        for name in ("pe", "act", "dve", "pool", "sp"):
            key, sem = self.newsem()
            self.eng[name] = dict(ops=[], cnt=0, sem=sem, key=key, waited={})

    def newsem(self):
        s = self.es.enter_context(self.nc.semaphore(f"sm{self.nsem}"))
        self.nsem += 1
        return self.nsem - 1, s

    @staticmethod
    def _merge(dst, key, sem, val):
        if key not in dst or dst[key][1] < val:
            dst[key] = (sem, val)

    def op(self, en, fn, reads=(), writes=(), inc=True, dma=None):
        E = self.eng[en]
        deps = {}
        for r in reads:
            for k, (s, v) in r.w.items():
                self._merge(deps, k, s, v)
        for w in writes:
            for k, (s, v) in w.r.items():
                self._merge(deps, k, s, v)
            for k, (s, v) in w.w.items():
                self._merge(deps, k, s, v)
        if dma is not None:
            if not hasattr(dma, "dq"):
                dma.dq = {}
            if en not in dma.dq:
                k_, s_ = self.newsem()
                dma.dq[en] = [k_, s_, 0]
                self.dma_res.append(dma.dq[en])
            d_ = dma.dq[en]
            d_[2] += 16
            ev = (d_[0], d_[1], d_[2])
            incspec = (d_[1], 16)
        elif inc:
            E["cnt"] += 1
            ev = (E["key"], E["sem"], E["cnt"])
            incspec = (E["sem"], 1)
        else:
            ev = (E["key"], E["sem"], E["cnt"] + 1)
            incspec = None
        waits = []
        for k, (s, v) in deps.items():
            if en == "pe" and k == E["key"] and dma is None:
                continue
            if E["waited"].get(k, 0) >= v:
                continue
            E["waited"][k] = v
            waits.append((s, v))
        E["ops"].append((waits, fn, incspec))
        for r in reads:
            self._merge(r.r, ev[0], ev[1], ev[2])
        for w in writes:
            self._merge(w.w, ev[0], ev[1], ev[2])
        if dma is None and inc and E["cnt"] >= self.EPOCH:
            E["key"], E["sem"] = self.newsem()
            E["cnt"] = 0
        return ev

    def barrier(self):
        evs = []
        for name, E in self.eng.items():
            if E["cnt"] > 0:
                evs.append((E["key"], E["sem"], E["cnt"]))
        for r in self.dma_res:
            if r[2] > 0:
                evs.append((r[0], r[1], r[2]))
        for name, E in self.eng.items():
            waits = []
            for k, s, v in evs:
                if k == E["key"]:
                    continue
                if E["waited"].get(k, 0) >= v:
                    continue
                E["waited"][k] = v
                waits.append((s, v))
            if waits:
                E["ops"].append((waits, None, None))

    def emit(self, en, handle):
        for waits, fn, incspec in self.eng[en]["ops"]:
            for s, v in waits:
                handle.wait_ge(s, v)
            if fn is None:
                continue
            ins = fn(handle)
            if incspec is not None:
                ins.then_inc(incspec[0], incspec[1])


class Tile(Res):
    uid = 0

    def __init__(self, ctx, name, shape, dt, psum=False):
        super().__init__()
        nc, es = ctx
        Tile.uid += 1
        name = f"{name}_{Tile.uid}"
        if psum:
            self.t = es.enter_context(nc.psum_tensor(name, shape, dt))
        else:
            self.t = es.enter_context(nc.sbuf_tensor(name, shape, dt))

    def __getitem__(self, k):
        return self.t[k]


class Rot:
    def __init__(self, tiles):
        self.tiles = tiles
        self.i = 0

    def next(self):
        t = self.tiles[self.i % len(self.tiles)]
        self.i += 1
        return t


def build(NBP, NBS):
    nc = bass.Bass("TRN2", target_bir_lowering=False)
    TP, TS = NBP * 128, NBS * 128
    TT = TP + TS

    def din(name, shape, dt=F32):
        return nc.dram_tensor(name, shape, dt, kind="ExternalInput").ap()

    def dout(name, shape, dt=F32):
        return nc.dram_tensor(name, shape, dt, kind="ExternalOutput").ap()

    xp = din("xp", [TP, D])
    xs = din("xs", [TS, D])
    w_in = din("w_in", [D, 2560])
    w_out = din("w_out", [D, D])
    wg = din("wg", [D, D])
    w_out2 = din("w_out2", [D, D])
    gvec = din("gvec", [128, 16])
    gf = din("gf", [128, D])
    biasT = din("biasT", [128, 12 * 512])
    sinkrep = din("sinkrep", [1, 2048])
    sinkcol = din("sinkcol", [128, 16])
    cs0 = din("cs0", [128, 1024])
    ident_d = din("ident", [128, 128], BF16)
    ones_d = din("ones", [128, 128], BF16)
    onesz_d = din("onesz", [128, 256], BF16)
    d1p = din("d1p", [NBP, 4 * NBP], BF16)
    d1s = din("d1s", [NBS, 4 * NBS], BF16)
    mp = din("mp", [128, NBP, 256], BF16)
    ms = din("ms", [128, NBS, 256], BF16)
    yp = dout("yp", [TP, D])
    ys = dout("ys", [TS, D])
    x1d = nc.dram_tensor("x1d", [TT, D], F32).ap()
    sgd = nc.dram_tensor("sgd", [TT, D], BF16).ap()
    abd = nc.dram_tensor("abd", [8, TT, 256], BF16).ap()
    fd = nc.dram_tensor("fd", [TT, D], BF16).ap()

    seqs = [(xp, yp, 0, NBP, d1p, mp), (xs, ys, TP, NBS, d1s, ms)]
    STOP = os.environ.get('KSTOP', '')
    ORDER = ['prep', 'A', 'A2', 'B', 'C', '']
    def active(ph):
        return ORDER.index(ph) <= ORDER.index(STOP)

    with contextlib.ExitStack() as es_all:
        S = Sched(nc, es_all)
        block_cm = nc.Block()

        R_x1 = [Res() for _ in range(TT // 128)]
        R_sg = [Res() for _ in range(TT // 128)]
        R_ab = [Res(), Res()]
        R_f = [Res(), Res()]
        out_events = []
        ss2_all = Tile((nc, es_all), "ss2_all", [128, TT // 128], F32)
        rstd2_all = Tile((nc, es_all), "rstd2_all", [128, TT // 128], F32)

        def mk_ps(es):
            ctx = (nc, es)
            psT = Tile(ctx, "psT", [128, 1024], BF16, psum=True)
            psA = Rot([Tile(ctx, f"psA{i}", [128, 512], F32, psum=True) for i in range(3)])
            psO = [Tile(ctx, f"psO{i}", [128, 512], F32, psum=True) for i in range(2)]
            psX = [Tile(ctx, f"psX{i}", [128, 512], F32, psum=True) for i in range(2)]
            return psT, psA, psO, psX

        def load(dst, dst_ap, src_ap, reads=(), q="sp"):
            S.op(q, lambda e, o=dst_ap, i=src_ap: e.dma_start(out=o, in_=i), reads=reads, writes=[dst], dma=dst)

        def store(src, dst_ap, src_ap, writes=(), q="sp"):
            return S.op(q, lambda e, o=dst_ap, i=src_ap: e.dma_start(out=o, in_=i), reads=[src], writes=writes, dma=src)

        def mm(out_t, out_ap, lhsT_t, lhsT_ap, rhs_t, rhs_ap, start, stop, extra_reads=()):
            S.op("pe", lambda e, o=out_ap, l=lhsT_ap, r=rhs_ap, a=start, b=stop: e.matmul(o, l, r, start=a, stop=b),
                 reads=[lhsT_t, rhs_t] + list(extra_reads), writes=[out_t], inc=bool(stop))

        def transposes(psT, src_t, src_fn, ident):
            for k in range(8):
                S.op("pe", lambda e, o=psT[:, k * 128:(k + 1) * 128], i=src_fn(k), idn=ident[:, :]: e.transpose(o, i, idn),
                     reads=[src_t, ident], writes=[psT], inc=(k == 7))

        def rms_rstd(ctx_tiles, x_t, x_ap, tag):
            junk, ss, rstd = ctx_tiles
            S.op("act", lambda e, o=junk[:, :], i=x_ap, a=ss[:, 0:1]: e.activation(o, i, AF.Square, accum_out=a),
                 reads=[x_t], writes=[junk, ss])
            S.op("dve", lambda e, o=rstd[:, 0:1], i=ss[:, 0:1]: e.tensor_scalar(o, i, 1.0 / D, EPS, ALU.mult, ALU.add),
                 reads=[ss], writes=[rstd])
            S.op("act", lambda e, o=rstd[:, 0:1], i=rstd[:, 0:1]: e.activation(o, i, AF.Ln),
                 reads=[rstd], writes=[rstd])
            S.op("act", lambda e, o=rstd[:, 0:1], i=rstd[:, 0:1]: e.activation(o, i, AF.Exp, scale=-0.5),
                 reads=[rstd], writes=[rstd])
            return rstd

        with contextlib.ExitStack() as es:
            ctx = (nc, es)
            psT, psA, psO, psX = mk_ps(es)
            Win = Tile(ctx, "Win", [128, 8, 2560], BF16)
            Wout = Tile(ctx, "Wout", [128, 8, 1024], BF16)
            x1t = [Tile(ctx, f"x1t{i}", [128, 1024], F32) for i in range(2)]
            stage = x1t
            gv = Tile(ctx, "gv", [128, 16], F32)
            esc = Tile(ctx, "esc", [128, 16], F32)
            biasb = Tile(ctx, "biasb", [128, 12, 512], BF16)
            esink = Tile(ctx, "esink", [128, 2048], BF16)
            ident = Tile(ctx, "ident", [128, 128], BF16)
            ones = Tile(ctx, "ones", [128, 128], BF16)
            onesz = Tile(ctx, "onesz", [128, 2, 128], BF16)
            xin = [Tile(ctx, f"xin{i}", [128, 1024], F32) for i in range(2)]
            xres = [Tile(ctx, f"xres{i}", [128, 1024], F32) for i in range(4)]
            junk = Tile(ctx, "junk", [128, 1024], BF16)
            ssr = [(junk, Tile(ctx, f"ss{i}", [128, 1], F32), Tile(ctx, f"rstd{i}", [128, 1], F32)) for i in range(2)]
            hb = [Tile(ctx, f"hb{i}", [128, 1024], BF16) for i in range(2)]
            hT = [Tile(ctx, f"hT{i}", [128, 8, 512], BF16) for i in range(2)]
            QT = Tile(ctx, "QT", [128, 4, 8, 128], BF16)
            GT = Tile(ctx, "GT", [128, 8, 512], BF16)
            KT = [Tile(ctx, f"KT{i}", [128, 2, 2, 512], BF16) for i in range(3)]
            VV = [Tile(ctx, f"VV{i}", [128, 4, 4, 128], BF16) for i in range(3)]
            PT = Rot([Tile(ctx, f"PT{i}", [128, 512], BF16) for i in range(12)])
            rden_l = [Tile(ctx, f"rden{i}", [128, 512], F32) for i in range(2)]
            tt_l = [Tile(ctx, f"tt{i}", [128, 512], F32) for i in range(2)]
            ogT = [Tile(ctx, f"ogT{i}", [128, 8, 128], BF16) for i in range(2)]

            load(gv, gv[:, :], gvec[:, :])
            load(esc, esc[:, :], sinkcol[:, :])
            S.op("act", lambda e: e.activation(esc[:, :], esc[:, :], AF.Exp), reads=[esc], writes=[esc])
            load(ident, ident[:, :], ident_d[:, :])
            load(ones, ones[:, :], ones_d[:, :])
            load(onesz, onesz[:, :, :], onesz_d.rearrange("p (a c) -> p a c", a=2))
            for vv_ in VV:
                S.op("dve", lambda e, o=vv_[:, :, :, :]: e.memset(o, 0.0), writes=[vv_])
            S.op("dve", lambda e: e.memset(esink[:, :], 0.0), writes=[esink])
            for hh_ in range(2):
                st = stage[hh_]
                load(st, st[0:1, :], sinkrep[:, hh_ * 1024:(hh_ + 1) * 1024])
                S.op("act", lambda e, o=esink[0:1, hh_ * 1024:(hh_ + 1) * 1024], i=st[0:1, :]: e.activation(o, i, AF.Exp),
                     reads=[st], writes=[esink])
            for kt_ in KT:
                S.op("dve", lambda e, o=kt_[:, :, :, :]: e.memset(o, 0.0), writes=[kt_])
            si = 0
            for k in range(8):
                for (c0_, c1_) in ((0, 1024), (1024, 2048), (2048, 2560)):
                    st = stage[si % 2]
                    si += 1
                    load(st, st[:, 0:c1_ - c0_], w_in[k * 128:(k + 1) * 128, c0_:c1_])
                    S.op("dve", lambda e, o=Win[:, k, c0_:c1_], i=st[:, 0:c1_ - c0_], g=gv[:, k:k + 1]:
                         e.tensor_scalar(o, i, g, None, ALU.mult), reads=[st, gv], writes=[Win])
                st = stage[si % 2]
                si += 1
                load(st, st[:, :], w_out[k * 128:(k + 1) * 128, :])
                S.op("act", lambda e, o=Wout[:, k, :], i=st[:, :]: e.copy(o, i), reads=[st], writes=[Wout])
            for jj in range(12):
                st = stage[si % 2]
                si += 1
                load(st, st[:, 0:512], biasT[:, jj * 512:(jj + 1) * 512])
                S.op("act", lambda e, o=biasb[:, jj, :], i=st[:, 0:512]: e.copy(o, i), reads=[st], writes=[biasb])

            for (xd, yd, tok0, NB, d1c, mc) in (seqs if active('A') else []):
                NG = NB // 4

                def front_a(b):
                    xi = xin[b % 2]
                    load(xi, xi[:, :], xd[b * 128:(b + 1) * 128, :])
                    rstd = rms_rstd(ssr[b % 2], xi, xi[:, :], "a")
                    h = hb[b % 2]
                    S.op("dve", lambda e, o=h[:, :], i=xi[:, :], r=rstd[:, 0:1]: e.tensor_scalar(o, i, r, None, ALU.mult),
                         reads=[xi, rstd], writes=[h])

                def front_b(b):
                    G, t = b // 4, b % 4
                    hTg = hT[G % 2]
                    h = hb[b % 2]
                    transposes(psT, h, lambda k, h=h: h[:, k * 128:(k + 1) * 128], ident)
                    S.op("dve", lambda e, o=hTg[:, :, t * 128:(t + 1) * 128],
                         i=psT[:, :].rearrange("p (k t) -> p k t", k=8): e.tensor_copy(o, i),
                         reads=[psT], writes=[hTg])

                def front(b):
                    front_a(b)
                    front_b(b)

                def kvproj(G):
                    slot = G % 3
                    hTg = hT[G % 2]
                    for kc in range(2):
                        ps = psA.next()
                        for k in range(8):
                            mm(ps, ps[:, :], Win, Win[:, k, 1024 + kc * 128:1024 + (kc + 1) * 128], hTg, hTg[:, k, :], k == 0, k == 7)
                        for half in range(2):
                            P0 = 64 * half
                            S.op("dve", lambda e, o=KT[slot][P0:P0 + 64, half, kc, :], i=ps[P0:P0 + 64, :]: e.tensor_copy(o, i),
                                 reads=[ps], writes=[KT[slot]])
                    for t in range(4):
                        ps = psA.next()
                        for k in range(8):
                            mm(ps, ps[:, 0:256], hTg, hTg[:, k, t * 128:(t + 1) * 128], Win, Win[:, k, 2304:2560], k == 0, k == 7)
                        for g in range(4):
                            c0_ = 64 * (g % 2)
                            S.op("act", lambda e, o=VV[slot][:, t, g, c0_:c0_ + 64], i=ps[:, 64 * g:64 * g + 64]: e.copy(o, i),
                                 reads=[ps], writes=[VV[slot]])

                def qgproj(G):
                    hTg = hT[G % 2]
                    for c in range(8):
                        ps = psA.next()
                        for k in range(8):
                            mm(ps, ps[:, :], Win, Win[:, k, c * 128:(c + 1) * 128], hTg, hTg[:, k, :], k == 0, k == 7)
                        S.op("dve", lambda e, o=QT[:, :, c, :], i=ps[:, :].rearrange("p (t q) -> p t q", t=4):
                             e.tensor_scalar(o, i, 0.125, None, ALU.mult), reads=[ps], writes=[QT])
                    for c in range(8):
                        ps = psA.next()
                        for k in range(8):
                            mm(ps, ps[:, :], Win, Win[:, k, 1280 + c * 128:1280 + (c + 1) * 128], hTg, hTg[:, k, :], k == 0, k == 7)
                        S.op("act", lambda e, o=GT[:, c, :], i=ps[:, :]: e.activation(o, i, AF.Silu), reads=[ps], writes=[GT])

                PTS = {}

                def qk_step(b, g):
                    t = b % 4
                    kc, half = g // 2, g % 2
                    js = [j for j in (0, 1, 2) if 0 <= b + j - 1 < NB]
                    pts = []
                    for j in js:
                        kb = b + j - 1
                        slot = (kb // 4) % 3
                        kt = kb % 4
                        ps = psA.next()
                        mm(ps, ps[:, :], ident, ident[:, :], biasb, biasb[:, j * 4 + g, :], True, False)
                        mm(ps, ps[:, :], KT[slot], KT[slot][:, half, kc, kt * 128:(kt + 1) * 128],
                           QT, QT[:, t, 4 * kc:4 * kc + 4, :].rearrange("p a q -> p (a q)"), False, True)
                        pt = PT.next()
                        S.op("act", lambda e, o=pt[:, :], i=ps[:, :]: e.activation(o, i, AF.Exp), reads=[ps], writes=[pt])
                        pts.append((pt, slot, kt))
                    PTS[(b, g)] = pts

                def pv_step(b, kc):
                    t = b % 4
                    og = ogT[b % 2]
                    po, pd = psO if kc % 2 == 0 else psX
                    rden = rden_l[kc % 2]
                    tt = tt_l[kc % 2]
                    allp = []
                    for half in range(2):
                        g = 2 * kc + half
                        for (pt, slot, kt) in PTS.pop((b, g)):
                            allp.append((pt, slot, kt, g, half))
                    for idx, (pt, slot, kt, g, half) in enumerate(allp):
                        mm(po, po[:, :], VV[slot], VV[slot][:, kt, g, :], pt, pt[:, :], idx == 0, idx == len(allp) - 1)
                    for idx, (pt, slot, kt, g, half) in enumerate(allp):
                        mm(pd, pd[:, :], onesz, onesz[:, half, :], pt, pt[:, :], idx == 0, False)
                    for half in range(2):
                        g = 2 * kc + half
                        mm(pd, pd[:, :], onesz, onesz[:, half, :], esink, esink[:, g * 512:(g + 1) * 512], False, half == 1)
                    S.op("act", lambda e, o=rden[:, :], i=pd[:, :]: e.activation(o, i, AF.Ln), reads=[pd], writes=[rden])
                    S.op("act", lambda e, o=rden[:, :], i=rden[:, :]: e.activation(o, i, AF.Exp, scale=-1.0),
                         reads=[rden], writes=[rden])
                    S.op("dve", lambda e, o=tt[:, :], a=po[:, :], b_=rden[:, :]: e.tensor_tensor(o, a, b_, ALU.mult),
                         reads=[po, rden], writes=[tt])
                    S.op("pool", lambda e, o=og[:, 4 * kc:4 * kc + 4, :],
                         a=tt[:, :].rearrange("p (a q) -> p a q", a=4),
                         b_=GT[:, 4 * kc:4 * kc + 4, t * 128:(t + 1) * 128]: e.tensor_tensor(o, a, b_, ALU.mult),
                         reads=[tt, GT], writes=[og])

                def out_step(b):
                    og = ogT[b % 2]
                    xr = xres[b % 4]
                    load(xr, xr[:, :], xd[b * 128:(b + 1) * 128, :])
                    x1 = x1t[b % 2]
                    for hf in range(2):
                        ps = psA.next()
                        for c in range(8):
                            mm(ps, ps[:, :], og, og[:, c, :], Wout, Wout[:, c, hf * 512:(hf + 1) * 512], c == 0, c == 7)
                        S.op("dve", lambda e, o=x1[:, hf * 512:(hf + 1) * 512], a=ps[:, :], b_=xr[:, hf * 512:(hf + 1) * 512]:
                             e.tensor_tensor(o, a, b_, ALU.add), reads=[ps, xr], writes=[x1])
                    ti = tok0 // 128 + b
                    S.op("act", lambda e, o=junk[:, :], i=x1[:, :], a=ss2_all[:, ti:ti + 1]: e.activation(o, i, AF.Square, accum_out=a),
                         reads=[x1], writes=[junk, ss2_all])
                    store(x1, x1d[ti * 128:(ti + 1) * 128, :], x1[:, :], writes=[R_x1[ti]], q="pool")

                def attend_group(G, nextfront):
                    qs = [(4 * G + t, g) for t in range(4) for g in range(4)]
                    nq = len(qs)
                    emitted_q = 0

                    def emit_q_upto(n):
                        nonlocal emitted_q
                        while emitted_q < min(n, nq):
                            qk_step(*qs[emitted_q])
                            emitted_q += 1

                    for bi in range(4):
                        b = 4 * G + bi
                        for kc in range(2):
                            emit_q_upto(4 * bi + 2 * kc + 3)
                            if kc == 1 and nextfront is not None:
                                front_b(nextfront + bi)
                            pv_step(b, kc)
                            if kc == 1 and bi >= 1:
                                out_step(b - 1)
                            if kc == 0 and nextfront is not None:
                                front_a(nextfront + bi)
                    PENDING.append(4 * G + 3)

                PENDING = []
                for t in range(4):
                    front(t)
                for G in range(NG + 1):
                    if G < NG:
                        kvproj(G)
                    while PENDING:
                        out_step(PENDING.pop(0))
                    if G == 0 and NG > 1:
                        for t in range(4):
                            front(4 + t)
                    if G >= 1:
                        qgproj(G - 1)
                        attend_group(G - 1, 4 * (G + 1) if G + 1 < NG else None)
                while PENDING:
                    out_step(PENDING.pop(0))
            S.barrier()

        with contextlib.ExitStack() as es:
            ctx = (nc, es)
            psT, psA, psO, psX = mk_ps(es)
            Wg = Tile(ctx, "Wg", [128, 8, 1024], BF16)
            CS = Tile(ctx, "CS", [128, 8, 512], BF16)
            cs0t = Tile(ctx, "cs0t", [128, 2, 512], F32)
            stage = [Tile(ctx, f"stage{i}", [128, 1024], F32) for i in range(2)]
            gv = Tile(ctx, "gv", [128, 16], F32)
            ident = Tile(ctx, "ident", [128, 128], BF16)
            x1i = [Tile(ctx, f"x1i{i}", [128, 1024], F32) for i in range(4)]
            junk = Tile(ctx, "junk", [128, 1024], BF16)
            ssr = [(junk, Tile(ctx, f"ss{i}", [128, 1], F32), Tile(ctx, f"rstd{i}", [128, 1], F32)) for i in range(2)]
            h2 = [Tile(ctx, f"h2{i}", [128, 1024], BF16) for i in range(4)]
            h2T = [Tile(ctx, f"h2T{i}", [128, 8, 128], BF16) for i in range(3)]
            sg = [Tile(ctx, f"sg{i}", [128, 1024], BF16) for i in range(5)]
            AB = [Tile(ctx, f"AB{i}", [128, 8, 256], BF16) for i in range(6)]

            load(gv, gv[:, :], gvec[:, :])
            load(ident, ident[:, :], ident_d[:, :])
            load(cs0t, cs0t[:, :, :], cs0.rearrange("p (a c) -> p a c", a=2))
            for kk in range(8):
                S.op("dve", lambda e, o=CS[:, kk, :], i=cs0t[:, kk % 2, :], g=gv[:, 8 + kk:9 + kk]:
                     e.tensor_scalar(o, i, g, None, ALU.mult), reads=[cs0t, gv], writes=[CS])
            for k in range(8):
                st = stage[k % 2]
                load(st, st[:, :], wg[k * 128:(k + 1) * 128, :])
                S.op("dve", lambda e, o=Wg[:, k, :], i=st[:, :], g=gv[:, 8 + k:9 + k]:
                     e.tensor_scalar(o, i, g, None, ALU.mult), reads=[st, gv], writes=[Wg])

            S.op("dve", lambda e: e.tensor_scalar(rstd2_all[:, :], ss2_all[:, :], 1.0 / D, EPS, ALU.mult, ALU.add),
                 reads=[ss2_all], writes=[rstd2_all])
            S.op("act", lambda e: e.activation(rstd2_all[:, :], rstd2_all[:, :], AF.Ln), reads=[rstd2_all], writes=[rstd2_all])
            S.op("act", lambda e: e.activation(rstd2_all[:, :], rstd2_all[:, :], AF.Exp, scale=-0.5),
                 reads=[rstd2_all], writes=[rstd2_all])

            def a2_front(ti):
                xi = x1i[ti % 4]
                load(xi, xi[:, :], x1d[ti * 128:(ti + 1) * 128, :], reads=[R_x1[ti]])
                h = h2[ti % 4]
                S.op("dve", lambda e, o=h[:, :], i=xi[:, :], r=rstd2_all[:, ti:ti + 1]: e.tensor_scalar(o, i, r, None, ALU.mult),
                     reads=[xi, rstd2_all], writes=[h])

            def a2_front_b(ti):
                h = h2[ti % 4]
                transposes(psT, h, lambda k, h=h: h[:, k * 128:(k + 1) * 128], ident)
                hTt = h2T[ti % 3]
                S.op("act", lambda e, o=hTt[:, :, :], i=psT[:, :].rearrange("p (k t) -> p k t", k=8): e.copy(o, i),
                     reads=[psT], writes=[hTt])

            def a2_back(ti):
                sq = 0 if ti < NBP else 1
                hTt = h2T[ti % 3]
                sgt = sg[ti % 5]
                for hf in range(2):
                    ps = (psX if ti % 2 == 0 else psO)[hf]
                    for k in range(8):
                        mm(ps, ps[:, :], hTt, hTt[:, k, :], Wg, Wg[:, k, hf * 512:(hf + 1) * 512], k == 0, k == 7)
                    S.op("act", lambda e, o=sgt[:, hf * 512:(hf + 1) * 512], i=ps[:, :]: e.activation(o, i, AF.Silu),
                         reads=[ps], writes=[sgt])
                store(sgt, sgd[ti * 128:(ti + 1) * 128, :], sgt[:, :], writes=[R_sg[ti]], q="pool")
                abt = AB[ti % 6]
                for g in range(4):
                    ps = psA.next()
                    for u in range(2):
                        kk = 2 * g + u
                        mm(ps, ps[:, :], hTt, hTt[:, kk, :], CS, CS[:, kk, :], u == 0, u == 1)
                    o_ap = abt[:, 2 * g:2 * g + 2, :].rearrange("p h (ab c) -> p h ab c", ab=2)
                    i_ap = ps[:, :].rearrange("p (ab h c) -> p h ab c", ab=2, h=2)
                    if g % 2 == 0:
                        S.op("act", lambda e, o=o_ap, i=i_ap: e.copy(o, i), reads=[ps], writes=[abt])
                    else:
                        S.op("dve", lambda e, o=o_ap, i=i_ap: e.tensor_copy(o, i), reads=[ps], writes=[abt])
                store(abt, abd[:, ti * 128:(ti + 1) * 128, :].rearrange("h t c -> t h c"),
                      abt[:, :, :], writes=[R_ab[sq]], q="pool")

            NTall = TT // 128 if active('A2') else 0
            for ti in range(NTall + 3):
                if ti < NTall:
                    a2_front(ti)
                if 1 <= ti < NTall + 1:
                    a2_front_b(ti - 1)
                if 3 <= ti:
                    a2_back(ti - 3)
            S.barrier()

        with contextlib.ExitStack() as es:
            ctx = (nc, es)
            psT, psA, psO, psX = mk_ps(es)
            ABin = [Tile(ctx, "ABin0", [128, 128, 256], BF16)]
            Y = Tile(ctx, "Y", [128, max(NBP, NBS), 2, 128], BF16)
            KCmax = min(16, max(NBP, NBS))
            Mres = Tile(ctx, "Mres", [128, max(NBP, NBS), 256], BF16)
            Fsb = [Tile(ctx, f"Fsb{i}", [128, KCmax, 128], BF16) for i in range(3)]
            d1t = [Tile(ctx, "d1pt", [NBP, 4 * NBP], BF16), Tile(ctx, "d1st", [NBS, 4 * NBS], BF16)]
            ai = 0
            mi = 0
            ps_all = Rot(psA.tiles + psO + psX)
            for sq, (xd, yd, tok0, N1, d1c, mc) in (enumerate(seqs) if active('B') else []):
                d1 = d1t[sq]
                load(d1, d1[:, :], d1c[:, :])
                for k0 in range(0, N1, 32):
                    kw = min(32, N1 - k0)
                    load(Mres, Mres[:, k0:k0 + kw, :], mc[:, k0:k0 + kw, :], q="sp")
                KC = min(16, N1)
                nchunk = N1 // KC
                CPB = min(64, 512 // (2 * N1))
                SEQ = N1 * 128
                fd_seq = fd[tok0:tok0 + SEQ, :].rearrange("(k2 k1) c -> k2 k1 c", k1=N1)
                for j in range(8):
                    ab = ABin[0]
                    ab_src = abd[j, tok0:tok0 + SEQ, :].rearrange("(n1 n2) c -> n1 n2 c", n2=128)
                    for q4 in range(4):
                        load(ab, ab[0:N1, 32 * q4:32 * q4 + 32, :], ab_src[:, 32 * q4:32 * q4 + 32, :],
                             reads=[R_ab[sq]])
                    for c0 in range(0, 128, CPB):
                        ps = ps_all.next()
                        for cc in range(CPB):
                            c = c0 + cc
                            o = ps[:, cc * 2 * N1:(cc + 1) * 2 * N1]
                            mm(ps, o, ab, ab[0:N1, :, c], d1, d1[0:N1, 0:2 * N1], True, False)
                            mm(ps, o, ab, ab[0:N1, :, 128 + c], d1, d1[0:N1, 2 * N1:4 * N1], False, True)
                        eng = "act" if (c0 // CPB) % 2 == 0 else "dve"
                        o_ap = Y[:, 0:N1, :, c0:c0 + CPB]
                        i_ap = ps[:, 0:CPB * 2 * N1].rearrange("p (c v k) -> p k v c", c=CPB, v=2)
                        if eng == "act":
                            S.op("act", lambda e, o=o_ap, i=i_ap: e.copy(o, i), reads=[ps], writes=[Y])
                        else:
                            S.op("dve", lambda e, o=o_ap, i=i_ap: e.tensor_copy(o, i), reads=[ps], writes=[Y])
                    for kc in range(nchunk):
                        fsb = Fsb[mi % 3]
                        mi += 1
                        for k4 in range(0, KC, 4):
                            ps = ps_all.next()
                            for u in range(4):
                                k1 = kc * KC + k4 + u
                                o = ps[:, u * 128:(u + 1) * 128]
                                mm(ps, o, Mres, Mres[:, k1, 0:128], Y, Y[:, k1, 0, :], True, False)
                                mm(ps, o, Mres, Mres[:, k1, 128:256], Y, Y[:, k1, 1, :], False, True)
                            eng = "act" if (k4 // 4) % 2 == 0 else "dve"
                            o_ap = fsb[:, k4:k4 + 4, :]
                            i_ap = ps[:, :].rearrange("p (u c) -> p u c", u=4)
                            if eng == "act":
                                S.op("act", lambda e, o=o_ap, i=i_ap: e.copy(o, i), reads=[ps], writes=[fsb])
                            else:
                                S.op("dve", lambda e, o=o_ap, i=i_ap: e.tensor_copy(o, i), reads=[ps], writes=[fsb])
                        for k8 in range(0, KC, 8):
                            kw = min(8, KC - k8)
                            store(fsb, fd_seq[:, kc * KC + k8:kc * KC + k8 + kw, j * 128:(j + 1) * 128],
                                  fsb[:, k8:k8 + kw, :], writes=[R_f[sq]], q="pool")
            S.barrier()

        with contextlib.ExitStack() as es:
            ctx = (nc, es)
            psT, psA, psO, psX = mk_ps(es)
            Wo2 = Tile(ctx, "Wo2", [128, 8, 1024], BF16)
            stage = [Tile(ctx, f"stage{i}", [128, 1024], F32) for i in range(2)]
            gft = Tile(ctx, "gft", [128, 1024], F32)
            ident = Tile(ctx, "ident", [128, 128], BF16)
            x1i = [Tile(ctx, f"x1i{i}", [128, 1024], F32) for i in range(8)]
            Fi = [Tile(ctx, f"Fi{i}", [128, 1024], BF16) for i in range(6)]
            sgi = [Tile(ctx, f"sgi{i}", [128, 1024], BF16) for i in range(6)]
            fg = [Tile(ctx, f"fg{i}", [128, 1024], BF16) for i in range(4)]
            fgT = [Tile(ctx, f"fgT{i}", [128, 8, 128], BF16) for i in range(3)]
            x2 = [Tile(ctx, f"x2{i}", [128, 1024], F32) for i in range(4)]
            yt = [Tile(ctx, f"yt{i}", [128, 1024], F32) for i in range(2)]
            junk = Tile(ctx, "junk", [128, 1024], BF16)
            ssr = [(junk, Tile(ctx, f"ss{i}", [128, 1], F32), Tile(ctx, f"rstd{i}", [128, 1], F32)) for i in range(4)]

            load(ident, ident[:, :], ident_d[:, :])
            load(gft, gft[:, :], gf[:, :])
            for k in range(8):
                st = stage[k % 2]
                load(st, st[:, :], w_out2[k * 128:(k + 1) * 128, :])
                S.op("act", lambda e, o=Wo2[:, k, :], i=st[:, :]: e.copy(o, i), reads=[st], writes=[Wo2])

            def c_front(ti):
                sq = 0 if ti < NBP else 1
                xi = x1i[ti % 8]
                load(xi, xi[:, :], x1d[ti * 128:(ti + 1) * 128, :], reads=[R_x1[ti]])
                fi = Fi[ti % 6]
                load(fi, fi[:, :], fd[ti * 128:(ti + 1) * 128, :], reads=[R_f[sq]], q="pool")
                si_ = sgi[ti % 6]
                load(si_, si_[:, :], sgd[ti * 128:(ti + 1) * 128, :], reads=[R_sg[ti]], q="pool")
                f = fg[ti % 4]
                S.op("dve", lambda e, o=f[:, :], a=fi[:, :], b_=si_[:, :]: e.tensor_tensor(o, a, b_, ALU.mult),
                     reads=[fi, si_], writes=[f])

            def c_front_b(ti):
                f = fg[ti % 4]
                transposes(psT, f, lambda k, f=f: f[:, k * 128:(k + 1) * 128], ident)
                fT = fgT[ti % 3]
                S.op("act", lambda e, o=fT[:, :, :], i=psT[:, :].rearrange("p (k t) -> p k t", k=8): e.copy(o, i),
                     reads=[psT], writes=[fT])

            def c_back(ti, part):
                sq = 0 if ti < NBP else 1
                xd, yd, tok0, NB, _, _ = seqs[sq]
                lt = ti - tok0 // 128
                xi = x1i[ti % 8]
                fT = fgT[ti % 3]
                x2t = x2[ti % 4]
                junk_, ss_, rstd_ = ssr[ti % 4]
                if part == "mm":
                    for hf in range(2):
                        ps = (psX if ti % 2 == 0 else psO)[hf]
                        for c in range(8):
                            mm(ps, ps[:, :], fT, fT[:, c, :], Wo2, Wo2[:, c, hf * 512:(hf + 1) * 512], c == 0, c == 7)
                    return
                if part == "post1":
                    for hf in range(2):
                        ps = (psX if ti % 2 == 0 else psO)[hf]
                        S.op("dve", lambda e, o=x2t[:, hf * 512:(hf + 1) * 512], a=ps[:, :], b_=xi[:, hf * 512:(hf + 1) * 512]:
                             e.tensor_tensor(o, a, b_, ALU.add), reads=[ps, xi], writes=[x2t])
                    S.op("act", lambda e, o=junk_[:, :], i=x2t[:, :], a=ss_[:, 0:1]: e.activation(o, i, AF.Square, accum_out=a),
                         reads=[x2t], writes=[junk_, ss_])
                    return
                if part == "post2":
                    S.op("dve", lambda e, o=rstd_[:, 0:1], i=ss_[:, 0:1]: e.tensor_scalar(o, i, 1.0 / D, EPS, ALU.mult, ALU.add),
                         reads=[ss_], writes=[rstd_])
                    S.op("act", lambda e, o=rstd_[:, 0:1], i=rstd_[:, 0:1]: e.activation(o, i, AF.Ln), reads=[rstd_], writes=[rstd_])
                    S.op("act", lambda e, o=rstd_[:, 0:1], i=rstd_[:, 0:1]: e.activation(o, i, AF.Exp, scale=-0.5),
                         reads=[rstd_], writes=[rstd_])
                    return
                y = yt[ti % 2]
                S.op("dve", lambda e, o=y[:, :], a=x2t[:, :], r=rstd_[:, 0:1], g=gft[:, :]:
                     e.scalar_tensor_tensor(o, a, r, g, ALU.mult, ALU.mult), reads=[x2t, rstd_, gft], writes=[y])
                ev = store(y, yd[lt * 128:(lt + 1) * 128, :], y[:, :], q="sp")
                out_events.append(ev)

            NTall = TT // 128 if active('C') else 0
            for ti in range(NTall + 6):
                if ti < NTall:
                    c_front(ti)
                if 1 <= ti < NTall + 1:
                    c_front_b(ti - 1)
                if 3 <= ti < NTall + 3:
                    c_back(ti - 3, "mm")
                if 4 <= ti < NTall + 4:
                    c_back(ti - 4, "post1")
                if 5 <= ti < NTall + 5:
                    c_back(ti - 5, "post2")
                if 6 <= ti:
                    c_back(ti - 6, "post3")
            S.barrier()

        fin = {}
        for k, s, v in out_events:
            if k not in fin or fin[k][1] < v:
                fin[k] = (s, v)
        S.eng["sp"]["ops"].append(([(s, v) for (s, v) in fin.values()], None, None))

        with block_cm as block:
            @block.tensor
            def _(e):
                S.emit("pe", e)

            @block.scalar
            def _(e):
                S.emit("act", e)

            @block.vector
            def _(e):
                S.emit("dve", e)

            @block.gpsimd
            def _(e):
                S.emit("pool", e)

            @block.sync
            def _(e):
                S.emit("sp", e)
    return nc


def _t5_bucket(rel):
    NUM_BUCKETS, MAX_DISTANCE = 32, 128
    half = NUM_BUCKETS // 2
    n = -rel
    ret = (n < 0).astype(np.int32) * half
    n = np.abs(n)
    max_exact = half // 2
    is_small = n < max_exact
    large = max_exact + (np.log(np.maximum(n, 1) / max_exact) / math.log(MAX_DISTANCE / max_exact)
                         * (half - max_exact)).astype(np.int32)
    large = np.minimum(large, half - 1)
    return (ret + np.where(is_small, n, large)).astype(np.int32)


def _consts(N1):
    bf = ml_dtypes.bfloat16
    n1 = np.arange(N1)
    ang = 2 * np.pi * ((n1[:, None] * n1[None, :]) % N1) / N1
    c, s = np.cos(ang), np.sin(ang)
    d1 = np.concatenate([c, s, -s, c], axis=1).astype(np.float32).astype(bf)
    SEQ = 128 * N1
    k1 = np.arange(N1)[:, None, None]
    n2 = np.arange(128)[None, :, None]
    k2 = np.arange(128)[None, None, :]
    idx = (n2 * (k1 + N1 * k2)) % SEQ
    th = 2 * np.pi * idx / SEQ
    sc = 1.0 / math.sqrt(SEQ)
    m = np.concatenate([np.cos(th) * sc, -np.sin(th) * sc], axis=2).astype(np.float32).astype(bf)
    m = np.transpose(m, (1, 0, 2))
    return np.ascontiguousarray(d1), np.ascontiguousarray(m)


def _shared_inputs(rel_bias, attn_norm, attn_w_in, attn_w_out, attn_sink, fourier_norm, fourier_w_gate,
                   fourier_w_out, final_norm, NBP, NBS):
    bf = ml_dtypes.bfloat16
    w = np.asarray(attn_w_in[0], np.float32)
    hp = []
    for kc in range(2):
        for i in range(4):
            hp += [8 * kc + i, 8 * kc + 4 + i]
    cols_q = np.concatenate([np.arange(64 * h, 64 * h + 64) for h in hp])
    q = w[:, cols_q]
    gg = w[:, 1536 + cols_q]
    kk = w[:, 1024:1280]
    vv = w[:, 1280:1536]
    w_in = np.ascontiguousarray(np.concatenate([q, kk, gg, vv], axis=1))
    w_out_p = np.ascontiguousarray(np.asarray(attn_w_out[0], np.float32)[cols_q, :])
    gvec = np.zeros((128, 16), np.float32)
    gvec[:, 0:8] = np.asarray(attn_norm[0], np.float32).reshape(8, 128).T
    gvec[:, 8:16] = np.asarray(fourier_norm[0], np.float32).reshape(8, 128).T
    gf = np.ascontiguousarray(np.broadcast_to(np.asarray(final_norm, np.float32)[None, :], (128, D)))
    rb = np.concatenate([np.asarray(rel_bias, np.float32), np.full((1, 16), NEG, np.float32)], axis=0)
    key = np.arange(128)[:, None]
    qi = np.arange(128)[None, :]
    biasT = np.zeros((128, 12, 4, 128), np.float32)
    for j in range(3):
        rel = 128 * (j - 1) + key - qi
        bucket = _t5_bucket(rel)
        bidx = np.where(np.abs(rel) <= 128, bucket, 32)
        for g in range(4):
            for hh in range(4):
                biasT[:, j * 4 + g, hh, :] = rb[bidx, 4 * g + hh]
    biasT = np.ascontiguousarray(biasT.reshape(128, 12 * 512))
    sink = np.asarray(attn_sink[0], np.float32)
    sinkrep = np.zeros((1, 4, 4, 128), np.float32)
    for g in range(4):
        for hh in range(4):
            sinkrep[0, g, hh, :] = sink[4 * g + hh]
    sinkrep = np.ascontiguousarray(sinkrep.reshape(1, 2048))
    sinkcol = np.ascontiguousarray(np.broadcast_to(sink[None, :], (128, 16))).astype(np.float32)
    onesz = np.zeros((128, 2, 128), np.float32)
    onesz[:, 0, 0:64] = 1.0
    onesz[:, 1, 64:128] = 1.0
    onesz = np.ascontiguousarray(onesz.reshape(128, 256)).astype(bf)
    r = (128 * np.arange(2)[None, :, None] + np.arange(128)[:, None, None])
    cc = np.arange(256)[None, None, :]
    ang = 2 * np.pi * ((r * cc) % 256) / 256
    cs0 = np.concatenate([np.cos(ang) / 16.0, np.sin(ang) / 16.0], axis=2).astype(np.float32)
    cs0 = np.ascontiguousarray(cs0.reshape(128, 1024))
    d1p, mp = _consts(NBP)
    d1s, ms = _consts(NBS)
    return dict(
        w_in=w_in, w_out=w_out_p, onesz=onesz,
        wg=np.ascontiguousarray(np.asarray(fourier_w_gate[0], np.float32)),
        w_out2=np.ascontiguousarray(np.asarray(fourier_w_out[0], np.float32)),
        gvec=gvec, gf=gf, biasT=biasT, sinkrep=sinkrep, sinkcol=sinkcol, cs0=cs0,
        ident=np.eye(128, dtype=np.float32).astype(bf), ones=np.ones((128, 128), np.float32).astype(bf),
        d1p=d1p, d1s=d1s, mp=mp, ms=ms,
    )


def run(xps, xss, weights, NBP, NBS, ncores):
    nc = build(NBP, NBS)
    shared = _shared_inputs(NBP=NBP, NBS=NBS, **weights)
    in_maps = []
    for c in range(ncores):
        m = dict(shared)
        m["xp"] = np.ascontiguousarray(xps[c], dtype=np.float32)
        m["xs"] = np.ascontiguousarray(xss[c], dtype=np.float32)
        in_maps.append(m)
    res = run_bass_kernel_spmd(nc, in_maps, core_ids=list(range(ncores)), **({'trace': True} if os.environ.get('KTRACE') else {}))
    if os.environ.get('KTRACE'):
        print('EXEC_NS', res.exec_time_ns)
    return [r["yp"] for r in res.results], [r["ys"] for r in res.results]


def kernel(x_prompt, x_sample, rel_bias, attn_norm, attn_w_in, attn_w_out, attn_sink,
           fourier_norm, fourier_w_gate, fourier_w_out, final_norm):
    x_prompt = np.asarray(x_prompt, np.float32)
    x_sample = np.asarray(x_sample, np.float32)
    NBP, NBS = 128, 32
    weights = dict(rel_bias=np.asarray(rel_bias), attn_norm=np.asarray(attn_norm), attn_w_in=np.asarray(attn_w_in),
                   attn_w_out=np.asarray(attn_w_out), attn_sink=np.asarray(attn_sink),
                   fourier_norm=np.asarray(fourier_norm), fourier_w_gate=np.asarray(fourier_w_gate),
                   fourier_w_out=np.asarray(fourier_w_out), final_norm=np.asarray(final_norm))
    zp = np.zeros((NBP * 128, D), np.float32)
    zs = np.zeros((NBS * 128, D), np.float32)
    xps = [x_prompt[c] if c < 2 else zp for c in range(8)]
    xss = [x_sample[c] if c < 4 else zs for c in range(8)]
    yps, yss = run(xps, xss, weights, NBP, NBS, 8)
    y_prompt = np.stack([yps[0], yps[1]], axis=0).astype(np.float32)
    y_sample = np.stack([yss[c] for c in range(4)], axis=0).astype(np.float32)
    return (y_prompt, y_sample)
```

```python
import math
import os
import contextlib
import numpy as np
import ml_dtypes
import concourse.bass as bass
import concourse.mybir as mybir
from concourse.bass_utils import run_bass_kernel_spmd

F32 = mybir.dt.float32
BF16 = mybir.dt.bfloat16
AF = mybir.ActivationFunctionType
ALU = mybir.AluOpType
D = 1024
EPS = 1e-6
NEG = -30000.0


class Res:
    def __init__(self):
        self.w = {}
        self.r = {}
        self.dsem = None
        self.dkey = None
        self.dcnt = 0


class Sched:
    EPOCH = 12000

    def __init__(self, nc, es):
        self.nc = nc
        self.es = es
        self.nsem = 0
        self.eng = {}
        self.dma_res = []
        for name in ("pe", "act", "dve", "pool", "sp"):
            key, sem = self.newsem()
            self.eng[name] = dict(ops=[], cnt=0, sem=sem, key=key, waited={})

    def newsem(self):
        s = self.es.enter_context(self.nc.semaphore(f"sm{self.nsem}"))
        self.nsem += 1
        return self.nsem - 1, s

    @staticmethod
    def _merge(dst, key, sem, val):
        if key not in dst or dst[key][1] < val:
            dst[key] = (sem, val)

    def op(self, en, fn, reads=(), writes=(), inc=True, dma=None):
        E = self.eng[en]
        deps = {}
        for r in reads:
            for k, (s, v) in r.w.items():
                self._merge(deps, k, s, v)
        for w in writes:
            for k, (s, v) in w.r.items():
                self._merge(deps, k, s, v)
            for k, (s, v) in w.w.items():
                self._merge(deps, k, s, v)
        if dma is not None:
            if not hasattr(dma, "dq"):
                dma.dq = {}
            if en not in dma.dq:
                k_, s_ = self.newsem()
                dma.dq[en] = [k_, s_, 0]
                self.dma_res.append(dma.dq[en])
            d_ = dma.dq[en]
            d_[2] += 16
            ev = (d_[0], d_[1], d_[2])
            incspec = (d_[1], 16)
        elif inc:
            E["cnt"] += 1
            ev = (E["key"], E["sem"], E["cnt"])
            incspec = (E["sem"], 1)
        else:
            ev = (E["key"], E["sem"], E["cnt"] + 1)
            incspec = None
        waits = []
        for k, (s, v) in deps.items():
            if en == "pe" and k == E["key"] and dma is None:
                continue
            if E["waited"].get(k, 0) >= v:
                continue
            E["waited"][k] = v
            waits.append((s, v))
        E["ops"].append((waits, fn, incspec))
        for r in reads:
            self._merge(r.r, ev[0], ev[1], ev[2])
        for w in writes:
            self._merge(w.w, ev[0], ev[1], ev[2])
        if dma is None and inc and E["cnt"] >= self.EPOCH:
            E["key"], E["sem"] = self.newsem()
            E["cnt"] = 0
        return ev

    def barrier(self):
        evs = []
        for name, E in self.eng.items():
            if E["cnt"] > 0:
                evs.append((E["key"], E["sem"], E["cnt"]))
        for r in self.dma_res:
            if r[2] > 0:
                evs.append((r[0], r[1], r[2]))
        for name, E in self.eng.items():
            waits = []
            for k, s, v in evs:
                if k == E["key"]:
                    continue
                if E["waited"].get(k, 0) >= v:
                    continue
                E["waited"][k] = v
                waits.append((s, v))
            if waits:
                E["ops"].append((waits, None, None))

    def emit(self, en, handle):
        for waits, fn, incspec in self.eng[en]["ops"]:
            for s, v in waits:
                handle.wait_ge(s, v)
            if fn is None:
                continue
            ins = fn(handle)
            if incspec is not None:
                ins.then_inc(incspec[0], incspec[1])


class Tile(Res):
    uid = 0

    def __init__(self, ctx, name, shape, dt, psum=False):
        super().__init__()
        nc, es = ctx
        Tile.uid += 1
        name = f"{name}_{Tile.uid}"
        if psum:
            self.t = es.enter_context(nc.psum_tensor(name, shape, dt))
        else:
            self.t = es.enter_context(nc.sbuf_tensor(name, shape, dt))

    def __getitem__(self, k):
        return self.t[k]


class Rot:
    def __init__(self, tiles):
        self.tiles = tiles
        self.i = 0

    def next(self):
        t = self.tiles[self.i % len(self.tiles)]
        self.i += 1
        return t


def build(NBP, NBS):
    nc = bass.Bass("TRN2", target_bir_lowering=False)
    TP, TS = NBP * 128, NBS * 128
    TT = TP + TS

    def din(name, shape, dt=F32):
        return nc.dram_tensor(name, shape, dt, kind="ExternalInput").ap()

    def dout(name, shape, dt=F32):
        return nc.dram_tensor(name, shape, dt, kind="ExternalOutput").ap()

    xp = din("xp", [TP, D])
    xs = din("xs", [TS, D])
    w_in = din("w_in", [D, 2560])
    w_out = din("w_out", [D, D])
    wg = din("wg", [D, D])
    w_out2 = din("w_out2", [D, D])
    gvec = din("gvec", [128, 16])
    gf = din("gf", [128, D])
    biasT = din("biasT", [128, 12 * 512])
    sinkrep = din("sinkrep", [1, 2048])
    sinkcol = din("sinkcol", [128, 16])
    cs0 = din("cs0", [128, 1024])
    ident_d = din("ident", [128, 128], BF16)
    ones_d = din("ones", [128, 128], BF16)
    onesz_d = din("onesz", [128, 256], BF16)
    d1p = din("d1p", [NBP, 4 * NBP], BF16)
    d1s = din("d1s", [NBS, 4 * NBS], BF16)
    mp = din("mp", [128, NBP, 256], BF16)
    ms = din("ms", [128, NBS, 256], BF16)
    yp = dout("yp", [TP, D])
    ys = dout("ys", [TS, D])
    x1d = nc.dram_tensor("x1d", [TT, D], F32).ap()
    sgd = nc.dram_tensor("sgd", [TT, D], BF16).ap()
    abd = nc.dram_tensor("abd", [8, TT, 256], BF16).ap()
    fd = nc.dram_tensor("fd", [TT, D], BF16).ap()

    seqs = [(xp, yp, 0, NBP, d1p, mp), (xs, ys, TP, NBS, d1s, ms)]
    STOP = os.environ.get('KSTOP', '')
    ORDER = ['prep', 'A', 'A2', 'B', 'C', '']
    def active(ph):
        return ORDER.index(ph) <= ORDER.index(STOP)

    with contextlib.ExitStack() as es_all:
        S = Sched(nc, es_all)
        block_cm = nc.Block()

        R_x1 = [Res() for _ in range(TT // 128)]
        R_sg = [Res() for _ in range(TT // 128)]
        R_ab = [Res(), Res()]
        R_f = [Res(), Res()]
        out_events = []
        ss2_all = Tile((nc, es_all), "ss2_all", [128, TT // 128], F32)
        rstd2_all = Tile((nc, es_all), "rstd2_all", [128, TT // 128], F32)

        def mk_ps(es):
            ctx = (nc, es)
            psT = Tile(ctx, "psT", [128, 1024], BF16, psum=True)
            psA = Rot([Tile(ctx, f"psA{i}", [128, 512], F32, psum=True) for i in range(3)])
            psO = [Tile(ctx, f"psO{i}", [128, 512], F32, psum=True) for i in range(2)]
            psX = [Tile(ctx, f"psX{i}", [128, 512], F32, psum=True) for i in range(2)]
            return psT, psA, psO, psX

        def load(dst, dst_ap, src_ap, reads=(), q="sp"):
            S.op(q, lambda e, o=dst_ap, i=src_ap: e.dma_start(out=o, in_=i), reads=reads, writes=[dst], dma=dst)

        def store(src, dst_ap, src_ap, writes=(), q="sp"):
            return S.op(q, lambda e, o=dst_ap, i=src_ap: e.dma_start(out=o, in_=i), reads=[src], writes=writes, dma=src)

        def mm(out_t, out_ap, lhsT_t, lhsT_ap, rhs_t, rhs_ap, start, stop, extra_reads=()):
            S.op("pe", lambda e, o=out_ap, l=lhsT_ap, r=rhs_ap, a=start, b=stop: e.matmul(o, l, r, start=a, stop=b),
                 reads=[lhsT_t, rhs_t] + list(extra_reads), writes=[out_t], inc=bool(stop))

        def transposes(psT, src_t, src_fn, ident):
            for k in range(8):
                S.op("pe", lambda e, o=psT[:, k * 128:(k + 1) * 128], i=src_fn(k), idn=ident[:, :]: e.transpose(o, i, idn),
                     reads=[src_t, ident], writes=[psT], inc=(k == 7))

        def rms_rstd(ctx_tiles, x_t, x_ap, tag):
            junk, ss, rstd = ctx_tiles
            S.op("act", lambda e, o=junk[:, :], i=x_ap, a=ss[:, 0:1]: e.activation(o, i, AF.Square, accum_out=a),
                 reads=[x_t], writes=[junk, ss])
            S.op("dve", lambda e, o=rstd[:, 0:1], i=ss[:, 0:1]: e.tensor_scalar(o, i, 1.0 / D, EPS, ALU.mult, ALU.add),
                 reads=[ss], writes=[rstd])
            S.op("act", lambda e, o=rstd[:, 0:1], i=rstd[:, 0:1]: e.activation(o, i, AF.Ln),
                 reads=[rstd], writes=[rstd])
            S.op("act", lambda e, o=rstd[:, 0:1], i=rstd[:, 0:1]: e.activation(o, i, AF.Exp, scale=-0.5),
                 reads=[rstd], writes=[rstd])
            return rstd

        with contextlib.ExitStack() as es:
            ctx = (nc, es)
            psT, psA, psO, psX = mk_ps(es)
            Win = Tile(ctx, "Win", [128, 8, 2560], BF16)
            Wout = Tile(ctx, "Wout", [128, 8, 1024], BF16)
            x1t = [Tile(ctx, f"x1t{i}", [128, 1024], F32) for i in range(2)]
            stage = x1t
            gv = Tile(ctx, "gv", [128, 16], F32)
            esc = Tile(ctx, "esc", [128, 16], F32)
            biasb = Tile(ctx, "biasb", [128, 12, 512], BF16)
            esink = Tile(ctx, "esink", [128, 2048], BF16)
            ident = Tile(ctx, "ident", [128, 128], BF16)
            ones = Tile(ctx, "ones", [128, 128], BF16)
            onesz = Tile(ctx, "onesz", [128, 2, 128], BF16)
            xin = [Tile(ctx, f"xin{i}", [128, 1024], F32) for i in range(2)]
            xres = [Tile(ctx, f"xres{i}", [128, 1024], F32) for i in range(4)]
            junk = Tile(ctx, "junk", [128, 1024], BF16)
            ssr = [(junk, Tile(ctx, f"ss{i}", [128, 1], F32), Tile(ctx, f"rstd{i}", [128, 1], F32)) for i in range(2)]
            hb = [Tile(ctx, f"hb{i}", [128, 1024], BF16) for i in range(2)]
            hT = [Tile(ctx, f"hT{i}", [128, 8, 512], BF16) for i in range(2)]
            QT = Tile(ctx, "QT", [128, 4, 8, 128], BF16)
            GT = Tile(ctx, "GT", [128, 8, 512], BF16)
            KT = [Tile(ctx, f"KT{i}", [128, 2, 2, 512], BF16) for i in range(3)]
            VV = [Tile(ctx, f"VV{i}", [128, 4, 4, 128], BF16) for i in range(3)]
            PT = Rot([Tile(ctx, f"PT{i}", [128, 512], BF16) for i in range(12)])
            rden_l = [Tile(ctx, f"rden{i}", [128, 512], F32) for i in range(2)]
            tt_l = [Tile(ctx, f"tt{i}", [128, 512], F32) for i in range(2)]
            ogT = [Tile(ctx, f"ogT{i}", [128, 8, 128], BF16) for i in range(2)]

            load(gv, gv[:, :], gvec[:, :])
            load(esc, esc[:, :], sinkcol[:, :])
            S.op("act", lambda e: e.activation(esc[:, :], esc[:, :], AF.Exp), reads=[esc], writes=[esc])
            load(ident, ident[:, :], ident_d[:, :])
            load(ones, ones[:, :], ones_d[:, :])
            load(onesz, onesz[:, :, :], onesz_d.rearrange("p (a c) -> p a c", a=2))
            for vv_ in VV:
                S.op("dve", lambda e, o=vv_[:, :, :, :]: e.memset(o, 0.0), writes=[vv_])
            S.op("dve", lambda e: e.memset(esink[:, :], 0.0), writes=[esink])
            for hh_ in range(2):
                st = stage[hh_]
                load(st, st[0:1, :], sinkrep[:, hh_ * 1024:(hh_ + 1) * 1024])
                S.op("act", lambda e, o=esink[0:1, hh_ * 1024:(hh_ + 1) * 1024], i=st[0:1, :]: e.activation(o, i, AF.Exp),
                     reads=[st], writes=[esink])
            for kt_ in KT:
                S.op("dve", lambda e, o=kt_[:, :, :, :]: e.memset(o, 0.0), writes=[kt_])
            si = 0
            for k in range(8):
                for (c0_, c1_) in ((0, 1024), (1024, 2048), (2048, 2560)):
                    st = stage[si % 2]
                    si += 1
                    load(st, st[:, 0:c1_ - c0_], w_in[k * 128:(k + 1) * 128, c0_:c1_])
                    S.op("dve", lambda e, o=Win[:, k, c0_:c1_], i=st[:, 0:c1_ - c0_], g=gv[:, k:k + 1]:
                         e.tensor_scalar(o, i, g, None, ALU.mult), reads=[st, gv], writes=[Win])
                st = stage[si % 2]
                si += 1
                load(st, st[:, :], w_out[k * 128:(k + 1) * 128, :])
                S.op("act", lambda e, o=Wout[:, k, :], i=st[:, :]: e.copy(o, i), reads=[st], writes=[Wout])
            for jj in range(12):
                st = stage[si % 2]
                si += 1
                load(st, st[:, 0:512], biasT[:, jj * 512:(jj + 1) * 512])
                S.op("act", lambda e, o=biasb[:, jj, :], i=st[:, 0:512]: e.copy(o, i), reads=[st], writes=[biasb])

            for (xd, yd, tok0, NB, d1c, mc) in (seqs if active('A') else []):
                NG = NB // 4

                def front_a(b):
                    xi = xin[b % 2]
                    load(xi, xi[:, :], xd[b * 128:(b + 1) * 128, :])
                    rstd = rms_rstd(ssr[b % 2], xi, xi[:, :], "a")
                    h = hb[b % 2]
                    S.op("dve", lambda e, o=h[:, :], i=xi[:, :], r=rstd[:, 0:1]: e.tensor_scalar(o, i, r, None, ALU.mult),
                         reads=[xi, rstd], writes=[h])

                def front_b(b):
                    G, t = b // 4, b % 4
                    hTg = hT[G % 2]
                    h = hb[b % 2]
                    transposes(psT, h, lambda k, h=h: h[:, k * 128:(k + 1) * 128], ident)
                    S.op("dve", lambda e, o=hTg[:, :, t * 128:(t + 1) * 128],
                         i=psT[:, :].rearrange("p (k t) -> p k t", k=8): e.tensor_copy(o, i),
                         reads=[psT], writes=[hTg])

                def front(b):
                    front_a(b)
                    front_b(b)

                def kvproj(G):
                    slot = G % 3
                    hTg = hT[G % 2]
                    for kc in range(2):
                        ps = psA.next()
                        for k in range(8):
                            mm(ps, ps[:, :], Win, Win[:, k, 1024 + kc * 128:1024 + (kc + 1) * 128], hTg, hTg[:, k, :], k == 0, k == 7)
                        for half in range(2):
                            P0 = 64 * half
                            S.op("dve", lambda e, o=KT[slot][P0:P0 + 64, half, kc, :], i=ps[P0:P0 + 64, :]: e.tensor_copy(o, i),
                                 reads=[ps], writes=[KT[slot]])
                    for t in range(4):
                        ps = psA.next()
                        for k in range(8):
                            mm(ps, ps[:, 0:256], hTg, hTg[:, k, t * 128:(t + 1) * 128], Win, Win[:, k, 2304:2560], k == 0, k == 7)
                        for g in range(4):
                            c0_ = 64 * (g % 2)
                            S.op("act", lambda e, o=VV[slot][:, t, g, c0_:c0_ + 64], i=ps[:, 64 * g:64 * g + 64]: e.copy(o, i),
                                 reads=[ps], writes=[VV[slot]])

                def qgproj(G):
                    hTg = hT[G % 2]
                    for c in range(8):
                        ps = psA.next()
                        for k in range(8):
                            mm(ps, ps[:, :], Win, Win[:, k, c * 128:(c + 1) * 128], hTg, hTg[:, k, :], k == 0, k == 7)
                        S.op("dve", lambda e, o=QT[:, :, c, :], i=ps[:, :].rearrange("p (t q) -> p t q", t=4):
                             e.tensor_scalar(o, i, 0.125, None, ALU.mult), reads=[ps], writes=[QT])
                    for c in range(8):
                        ps = psA.next()
                        for k in range(8):
                            mm(ps, ps[:, :], Win, Win[:, k, 1280 + c * 128:1280 + (c + 1) * 128], hTg, hTg[:, k, :], k == 0, k == 7)
                        S.op("act", lambda e, o=GT[:, c, :], i=ps[:, :]: e.activation(o, i, AF.Silu), reads=[ps], writes=[GT])

                PTS = {}

                def qk_step(b, g):
                    t = b % 4
                    kc, half = g // 2, g % 2
                    js = [j for j in (0, 1, 2) if 0 <= b + j - 1 < NB]
                    pts = []
                    for j in js:
                        kb = b + j - 1
                        slot = (kb // 4) % 3
                        kt = kb % 4
                        ps = psA.next()
                        mm(ps, ps[:, :], ident, ident[:, :], biasb, biasb[:, j * 4 + g, :], True, False)
                        mm(ps, ps[:, :], KT[slot], KT[slot][:, half, kc, kt * 128:(kt + 1) * 128],
                           QT, QT[:, t, 4 * kc:4 * kc + 4, :].rearrange("p a q -> p (a q)"), False, True)
                        pt = PT.next()
                        S.op("act", lambda e, o=pt[:, :], i=ps[:, :]: e.activation(o, i, AF.Exp), reads=[ps], writes=[pt])
                        pts.append((pt, slot, kt))
                    PTS[(b, g)] = pts

                def pv_step(b, kc):
                    t = b % 4
                    og = ogT[b % 2]
                    po, pd = psO if kc % 2 == 0 else psX
                    rden = rden_l[kc % 2]
                    tt = tt_l[kc % 2]
                    allp = []
                    for half in range(2):
                        g = 2 * kc + half
                        for (pt, slot, kt) in PTS.pop((b, g)):
                            allp.append((pt, slot, kt, g, half))
                    for idx, (pt, slot, kt, g, half) in enumerate(allp):
                        mm(po, po[:, :], VV[slot], VV[slot][:, kt, g, :], pt, pt[:, :], idx == 0, idx == len(allp) - 1)
                    for idx, (pt, slot, kt, g, half) in enumerate(allp):
                        mm(pd, pd[:, :], onesz, onesz[:, half, :], pt, pt[:, :], idx == 0, False)
                    for half in range(2):
                        g = 2 * kc + half
                        mm(pd, pd[:, :], onesz, onesz[:, half, :], esink, esink[:, g * 512:(g + 1) * 512], False, half == 1)
                    S.op("act", lambda e, o=rden[:, :], i=pd[:, :]: e.activation(o, i, AF.Ln), reads=[pd], writes=[rden])
                    S.op("act", lambda e, o=rden[:, :], i=rden[:, :]: e.activation(o, i, AF.Exp, scale=-1.0),
                         reads=[rden], writes=[rden])
                    S.op("dve", lambda e, o=tt[:, :], a=po[:, :], b_=rden[:, :]: e.tensor_tensor(o, a, b_, ALU.mult),
                         reads=[po, rden], writes=[tt])
                    S.op("pool", lambda e, o=og[:, 4 * kc:4 * kc + 4, :],
                         a=tt[:, :].rearrange("p (a q) -> p a q", a=4),
                         b_=GT[:, 4 * kc:4 * kc + 4, t * 128:(t + 1) * 128]: e.tensor_tensor(o, a, b_, ALU.mult),
                         reads=[tt, GT], writes=[og])

                def out_step(b):
                    og = ogT[b % 2]
                    xr = xres[b % 4]
                    load(xr, xr[:, :], xd[b * 128:(b + 1) * 128, :])
                    x1 = x1t[b % 2]
                    for hf in range(2):
                        ps = psA.next()
                        for c in range(8):
                            mm(ps, ps[:, :], og, og[:, c, :], Wout, Wout[:, c, hf * 512:(hf + 1) * 512], c == 0, c == 7)
                        S.op("dve", lambda e, o=x1[:, hf * 512:(hf + 1) * 512], a=ps[:, :], b_=xr[:, hf * 512:(hf + 1) * 512]:
                             e.tensor_tensor(o, a, b_, ALU.add), reads=[ps, xr], writes=[x1])
                    ti = tok0 // 128 + b
                    S.op("act", lambda e, o=junk[:, :], i=x1[:, :], a=ss2_all[:, ti:ti + 1]: e.activation(o, i, AF.Square, accum_out=a),
                         reads=[x1], writes=[junk, ss2_all])
                    PSTORE.append((x1, ti))
                    if len(PSTORE) > 1:
                        x1p, tip = PSTORE.pop(0)
                        store(x1p, x1d[tip * 128:(tip + 1) * 128, :], x1p[:, :], writes=[R_x1[tip]], q="pool")

                def attend_group(G, nextfront):
                    qs = [(4 * G + t, g) for t in range(4) for g in range(4)]
                    nq = len(qs)
                    emitted_q = 0

                    def emit_q_upto(n):
                        nonlocal emitted_q
                        while emitted_q < min(n, nq):
                            qk_step(*qs[emitted_q])
                            emitted_q += 1

                    for bi in range(4):
                        b = 4 * G + bi
                        for kc in range(2):
                            emit_q_upto(4 * bi + 2 * kc + 3)
                            if kc == 1 and nextfront is not None:
                                front_b(nextfront + bi)
                            pv_step(b, kc)
                            if kc == 1 and bi >= 1:
                                out_step(b - 1)
                            if kc == 0 and nextfront is not None:
                                front_a(nextfront + bi)
                    PENDING.append(4 * G + 3)

                PENDING = []
                PSTORE = []
                for t in range(4):
                    front(t)
                for G in range(NG + 1):
                    if G < NG:
                        kvproj(G)
                    while PENDING:
                        out_step(PENDING.pop(0))
                    if G == 0 and NG > 1:
                        for t in range(4):
                            front(4 + t)
                    if G >= 1:
                        qgproj(G - 1)
                        attend_group(G - 1, 4 * (G + 1) if G + 1 < NG else None)
                while PENDING:
                    out_step(PENDING.pop(0))
                while PSTORE:
                    x1p, tip = PSTORE.pop(0)
                    store(x1p, x1d[tip * 128:(tip + 1) * 128, :], x1p[:, :], writes=[R_x1[tip]], q="pool")
            S.barrier()

        with contextlib.ExitStack() as es:
            ctx = (nc, es)
            psT, psA, psO, psX = mk_ps(es)
            Wg = Tile(ctx, "Wg", [128, 8, 1024], BF16)
            CS = Tile(ctx, "CS", [128, 8, 512], BF16)
            cs0t = Tile(ctx, "cs0t", [128, 2, 512], F32)
            stage = [Tile(ctx, f"stage{i}", [128, 1024], F32) for i in range(2)]
            gv = Tile(ctx, "gv", [128, 16], F32)
            ident = Tile(ctx, "ident", [128, 128], BF16)
            x1i = [Tile(ctx, f"x1i{i}", [128, 1024], F32) for i in range(4)]
            junk = Tile(ctx, "junk", [128, 1024], BF16)
            ssr = [(junk, Tile(ctx, f"ss{i}", [128, 1], F32), Tile(ctx, f"rstd{i}", [128, 1], F32)) for i in range(2)]
            h2 = [Tile(ctx, f"h2{i}", [128, 1024], BF16) for i in range(4)]
            h2T = [Tile(ctx, f"h2T{i}", [128, 8, 128], BF16) for i in range(3)]
            sg = [Tile(ctx, f"sg{i}", [128, 1024], BF16) for i in range(5)]
            AB = [Tile(ctx, f"AB{i}", [128, 8, 256], BF16) for i in range(6)]

            load(gv, gv[:, :], gvec[:, :])
            load(ident, ident[:, :], ident_d[:, :])
            load(cs0t, cs0t[:, :, :], cs0.rearrange("p (a c) -> p a c", a=2))
            for kk in range(8):
                S.op("dve", lambda e, o=CS[:, kk, :], i=cs0t[:, kk % 2, :], g=gv[:, 8 + kk:9 + kk]:
                     e.tensor_scalar(o, i, g, None, ALU.mult), reads=[cs0t, gv], writes=[CS])
            for k in range(8):
                st = stage[k % 2]
                load(st, st[:, :], wg[k * 128:(k + 1) * 128, :])
                S.op("dve", lambda e, o=Wg[:, k, :], i=st[:, :], g=gv[:, 8 + k:9 + k]:
                     e.tensor_scalar(o, i, g, None, ALU.mult), reads=[st, gv], writes=[Wg])

            S.op("dve", lambda e: e.tensor_scalar(rstd2_all[:, :], ss2_all[:, :], 1.0 / D, EPS, ALU.mult, ALU.add),
                 reads=[ss2_all], writes=[rstd2_all])
            S.op("act", lambda e: e.activation(rstd2_all[:, :], rstd2_all[:, :], AF.Ln), reads=[rstd2_all], writes=[rstd2_all])
            S.op("act", lambda e: e.activation(rstd2_all[:, :], rstd2_all[:, :], AF.Exp, scale=-0.5),
                 reads=[rstd2_all], writes=[rstd2_all])

            def a2_front(ti):
                xi = x1i[ti % 4]
                load(xi, xi[:, :], x1d[ti * 128:(ti + 1) * 128, :], reads=[R_x1[ti]])
                h = h2[ti % 4]
                S.op("dve", lambda e, o=h[:, :], i=xi[:, :], r=rstd2_all[:, ti:ti + 1]: e.tensor_scalar(o, i, r, None, ALU.mult),
                     reads=[xi, rstd2_all], writes=[h])

            def a2_front_b(ti):
                h = h2[ti % 4]
                transposes(psT, h, lambda k, h=h: h[:, k * 128:(k + 1) * 128], ident)
                hTt = h2T[ti % 3]
                S.op("act", lambda e, o=hTt[:, :, :], i=psT[:, :].rearrange("p (k t) -> p k t", k=8): e.copy(o, i),
                     reads=[psT], writes=[hTt])

            def a2_back(ti):
                sq = 0 if ti < NBP else 1
                hTt = h2T[ti % 3]
                sgt = sg[ti % 5]
                for hf in range(2):
                    ps = (psX if ti % 2 == 0 else psO)[hf]
                    for k in range(8):
                        mm(ps, ps[:, :], hTt, hTt[:, k, :], Wg, Wg[:, k, hf * 512:(hf + 1) * 512], k == 0, k == 7)
                    S.op("act", lambda e, o=sgt[:, hf * 512:(hf + 1) * 512], i=ps[:, :]: e.activation(o, i, AF.Silu),
                         reads=[ps], writes=[sgt])
                store(sgt, sgd[ti * 128:(ti + 1) * 128, :], sgt[:, :], writes=[R_sg[ti]], q="pool")
                abt = AB[ti % 6]
                for g in range(4):
                    ps = psA.next()
                    for u in range(2):
                        kk = 2 * g + u
                        mm(ps, ps[:, :], hTt, hTt[:, kk, :], CS, CS[:, kk, :], u == 0, u == 1)
                    o_ap = abt[:, 2 * g:2 * g + 2, :].rearrange("p h (ab c) -> p h ab c", ab=2)
                    i_ap = ps[:, :].rearrange("p (ab h c) -> p h ab c", ab=2, h=2)
                    if g % 2 == 0:
                        S.op("act", lambda e, o=o_ap, i=i_ap: e.copy(o, i), reads=[ps], writes=[abt])
                    else:
                        S.op("dve", lambda e, o=o_ap, i=i_ap: e.tensor_copy(o, i), reads=[ps], writes=[abt])
                store(abt, abd[:, ti * 128:(ti + 1) * 128, :].rearrange("h t c -> t h c"),
                      abt[:, :, :], writes=[R_ab[sq]], q="pool")

            NTall = TT // 128 if active('A2') else 0
            for ti in range(NTall + 3):
                if ti < NTall:
                    a2_front(ti)
                if 1 <= ti < NTall + 1:
                    a2_front_b(ti - 1)
                if 3 <= ti:
                    a2_back(ti - 3)
            S.barrier()

        with contextlib.ExitStack() as es:
            ctx = (nc, es)
            psT, psA, psO, psX = mk_ps(es)
            ABin = [Tile(ctx, "ABin0", [128, 128, 256], BF16)]
            Y = Tile(ctx, "Y", [128, max(NBP, NBS), 2, 128], BF16)
            KCmax = min(16, max(NBP, NBS))
            Mres = Tile(ctx, "Mres", [128, max(NBP, NBS), 256], BF16)
            Fsb = [Tile(ctx, f"Fsb{i}", [128, KCmax, 128], BF16) for i in range(3)]
            d1t = [Tile(ctx, "d1pt", [NBP, 4 * NBP], BF16), Tile(ctx, "d1st", [NBS, 4 * NBS], BF16)]
            ai = 0
            mi = 0
            ps_all = Rot(psA.tiles + psO + psX)
            for sq, (xd, yd, tok0, N1, d1c, mc) in (enumerate(seqs) if active('B') else []):
                d1 = d1t[sq]
                load(d1, d1[:, :], d1c[:, :])
                for k0 in range(0, N1, 32):
                    kw = min(32, N1 - k0)
                    load(Mres, Mres[:, k0:k0 + kw, :], mc[:, k0:k0 + kw, :], q="sp")
                KC = min(16, N1)
                nchunk = N1 // KC
                CPB = min(64, 512 // (2 * N1))
                SEQ = N1 * 128
                fd_seq = fd[tok0:tok0 + SEQ, :].rearrange("(k2 k1) c -> k2 k1 c", k1=N1)
                for j in range(8):
                    ab = ABin[0]
                    ab_src = abd[j, tok0:tok0 + SEQ, :].rearrange("(n1 n2) c -> n1 n2 c", n2=128)
                    for q4 in range(4):
                        load(ab, ab[0:N1, 32 * q4:32 * q4 + 32, :], ab_src[:, 32 * q4:32 * q4 + 32, :],
                             reads=[R_ab[sq]])
                    for c0 in range(0, 128, CPB):
                        ps = ps_all.next()
                        for cc in range(CPB):
                            c = c0 + cc
                            o = ps[:, cc * 2 * N1:(cc + 1) * 2 * N1]
                            mm(ps, o, ab, ab[0:N1, :, c], d1, d1[0:N1, 0:2 * N1], True, False)
                            mm(ps, o, ab, ab[0:N1, :, 128 + c], d1, d1[0:N1, 2 * N1:4 * N1], False, True)
                        eng = "act" if (c0 // CPB) % 2 == 0 else "dve"
                        o_ap = Y[:, 0:N1, :, c0:c0 + CPB]
                        i_ap = ps[:, 0:CPB * 2 * N1].rearrange("p (c v k) -> p k v c", c=CPB, v=2)
                        if eng == "act":
                            S.op("act", lambda e, o=o_ap, i=i_ap: e.copy(o, i), reads=[ps], writes=[Y])
                        else:
                            S.op("dve", lambda e, o=o_ap, i=i_ap: e.tensor_copy(o, i), reads=[ps], writes=[Y])
                    for kc in range(nchunk):
                        fsb = Fsb[mi % 3]
                        mi += 1
                        for k4 in range(0, KC, 4):
                            ps = ps_all.next()
                            for u in range(4):
                                k1 = kc * KC + k4 + u
                                o = ps[:, u * 128:(u + 1) * 128]
                                mm(ps, o, Mres, Mres[:, k1, 0:128], Y, Y[:, k1, 0, :], True, False)
                                mm(ps, o, Mres, Mres[:, k1, 128:256], Y, Y[:, k1, 1, :], False, True)
                            eng = "act" if (k4 // 4) % 2 == 0 else "dve"
                            o_ap = fsb[:, k4:k4 + 4, :]
                            i_ap = ps[:, :].rearrange("p (u c) -> p u c", u=4)
                            if eng == "act":
                                S.op("act", lambda e, o=o_ap, i=i_ap: e.copy(o, i), reads=[ps], writes=[fsb])
                            else:
                                S.op("dve", lambda e, o=o_ap, i=i_ap: e.tensor_copy(o, i), reads=[ps], writes=[fsb])
                        for k8 in range(0, KC, 8):
                            kw = min(8, KC - k8)
                            store(fsb, fd_seq[:, kc * KC + k8:kc * KC + k8 + kw, j * 128:(j + 1) * 128],
                                  fsb[:, k8:k8 + kw, :], writes=[R_f[sq]], q="pool")
            S.barrier()

        with contextlib.ExitStack() as es:
            ctx = (nc, es)
            psT, psA, psO, psX = mk_ps(es)
            Wo2 = Tile(ctx, "Wo2", [128, 8, 1024], BF16)
            stage = [Tile(ctx, f"stage{i}", [128, 1024], F32) for i in range(2)]
            gft = Tile(ctx, "gft", [128, 1024], F32)
            ident = Tile(ctx, "ident", [128, 128], BF16)
            x1i = [Tile(ctx, f"x1i{i}", [128, 1024], F32) for i in range(8)]
            Fi = [Tile(ctx, f"Fi{i}", [128, 1024], BF16) for i in range(6)]
            sgi = [Tile(ctx, f"sgi{i}", [128, 1024], BF16) for i in range(6)]
            fg = [Tile(ctx, f"fg{i}", [128, 1024], BF16) for i in range(4)]
            fgT = [Tile(ctx, f"fgT{i}", [128, 8, 128], BF16) for i in range(3)]
            x2 = [Tile(ctx, f"x2{i}", [128, 1024], F32) for i in range(4)]
            yt = [Tile(ctx, f"yt{i}", [128, 1024], F32) for i in range(2)]
            junk = Tile(ctx, "junk", [128, 1024], BF16)
            ssr = [(junk, Tile(ctx, f"ss{i}", [128, 1], F32), Tile(ctx, f"rstd{i}", [128, 1], F32)) for i in range(4)]

            load(ident, ident[:, :], ident_d[:, :])
            load(gft, gft[:, :], gf[:, :])
            for k in range(8):
                st = stage[k % 2]
                load(st, st[:, :], w_out2[k * 128:(k + 1) * 128, :])
                S.op("act", lambda e, o=Wo2[:, k, :], i=st[:, :]: e.copy(o, i), reads=[st], writes=[Wo2])

            def c_front(ti):
                sq = 0 if ti < NBP else 1
                xi = x1i[ti % 8]
                load(xi, xi[:, :], x1d[ti * 128:(ti + 1) * 128, :], reads=[R_x1[ti]])
                fi = Fi[ti % 6]
                load(fi, fi[:, :], fd[ti * 128:(ti + 1) * 128, :], reads=[R_f[sq]], q="pool")
                si_ = sgi[ti % 6]
                load(si_, si_[:, :], sgd[ti * 128:(ti + 1) * 128, :], reads=[R_sg[ti]], q="pool")
                f = fg[ti % 4]
                S.op("dve", lambda e, o=f[:, :], a=fi[:, :], b_=si_[:, :]: e.tensor_tensor(o, a, b_, ALU.mult),
                     reads=[fi, si_], writes=[f])

            def c_front_b(ti):
                f = fg[ti % 4]
                transposes(psT, f, lambda k, f=f: f[:, k * 128:(k + 1) * 128], ident)
                fT = fgT[ti % 3]
                S.op("act", lambda e, o=fT[:, :, :], i=psT[:, :].rearrange("p (k t) -> p k t", k=8): e.copy(o, i),
                     reads=[psT], writes=[fT])

            def c_back(ti, part):
                sq = 0 if ti < NBP else 1
                xd, yd, tok0, NB, _, _ = seqs[sq]
                lt = ti - tok0 // 128
                xi = x1i[ti % 8]
                fT = fgT[ti % 3]
                x2t = x2[ti % 4]
                junk_, ss_, rstd_ = ssr[ti % 4]
                if part == "mm":
                    for hf in range(2):
                        ps = (psX if ti % 2 == 0 else psO)[hf]
                        for c in range(8):
                            mm(ps, ps[:, :], fT, fT[:, c, :], Wo2, Wo2[:, c, hf * 512:(hf + 1) * 512], c == 0, c == 7)
                    return
                if part == "post1":
                    for hf in range(2):
                        ps = (psX if ti % 2 == 0 else psO)[hf]
                        S.op("dve", lambda e, o=x2t[:, hf * 512:(hf + 1) * 512], a=ps[:, :], b_=xi[:, hf * 512:(hf + 1) * 512]:
                             e.tensor_tensor(o, a, b_, ALU.add), reads=[ps, xi], writes=[x2t])
                    S.op("act", lambda e, o=junk_[:, :], i=x2t[:, :], a=ss_[:, 0:1]: e.activation(o, i, AF.Square, accum_out=a),
                         reads=[x2t], writes=[junk_, ss_])
                    return
                if part == "post2":
                    S.op("dve", lambda e, o=rstd_[:, 0:1], i=ss_[:, 0:1]: e.tensor_scalar(o, i, 1.0 / D, EPS, ALU.mult, ALU.add),
                         reads=[ss_], writes=[rstd_])
                    S.op("act", lambda e, o=rstd_[:, 0:1], i=rstd_[:, 0:1]: e.activation(o, i, AF.Ln), reads=[rstd_], writes=[rstd_])
                    S.op("act", lambda e, o=rstd_[:, 0:1], i=rstd_[:, 0:1]: e.activation(o, i, AF.Exp, scale=-0.5),
                         reads=[rstd_], writes=[rstd_])
                    return
                y = yt[ti % 2]
                S.op("dve", lambda e, o=y[:, :], a=x2t[:, :], r=rstd_[:, 0:1], g=gft[:, :]:
                     e.scalar_tensor_tensor(o, a, r, g, ALU.mult, ALU.mult), reads=[x2t, rstd_, gft], writes=[y])
                ev = store(y, yd[lt * 128:(lt + 1) * 128, :], y[:, :], q="sp")
                out_events.append(ev)

            NTall = TT // 128 if active('C') else 0
            for ti in range(NTall + 6):
                if ti < NTall:
                    c_front(ti)
                if 1 <= ti < NTall + 1:
                    c_front_b(ti - 1)
                if 3 <= ti < NTall + 3:
                    c_back(ti - 3, "mm")
                if 4 <= ti < NTall + 4:
                    c_back(ti - 4, "post1")
                if 5 <= ti < NTall + 5:
                    c_back(ti - 5, "post2")
                if 6 <= ti:
                    c_back(ti - 6, "post3")
            S.barrier()

        fin = {}
        for k, s, v in out_events:
            if k not in fin or fin[k][1] < v:
                fin[k] = (s, v)
        S.eng["sp"]["ops"].append(([(s, v) for (s, v) in fin.values()], None, None))

        with block_cm as block:
            @block.tensor
            def _(e):
                S.emit("pe", e)

            @block.scalar
            def _(e):
                S.emit("act", e)

            @block.vector
            def _(e):
                S.emit("dve", e)

            @block.gpsimd
            def _(e):
                S.emit("pool", e)

            @block.sync
            def _(e):
                S.emit("sp", e)
    return nc


def _t5_bucket(rel):
    NUM_BUCKETS, MAX_DISTANCE = 32, 128
    half = NUM_BUCKETS // 2
    n = -rel
    ret = (n < 0).astype(np.int32) * half
    n = np.abs(n)
    max_exact = half // 2
    is_small = n < max_exact
    large = max_exact + (np.log(np.maximum(n, 1) / max_exact) / math.log(MAX_DISTANCE / max_exact)
                         * (half - max_exact)).astype(np.int32)
    large = np.minimum(large, half - 1)
    return (ret + np.where(is_small, n, large)).astype(np.int32)


def _consts(N1):
    bf = ml_dtypes.bfloat16
    n1 = np.arange(N1)
    ang = 2 * np.pi * ((n1[:, None] * n1[None, :]) % N1) / N1
    c, s = np.cos(ang), np.sin(ang)
    d1 = np.concatenate([c, s, -s, c], axis=1).astype(np.float32).astype(bf)
    SEQ = 128 * N1
    k1 = np.arange(N1)[:, None, None]
    n2 = np.arange(128)[None, :, None]
    k2 = np.arange(128)[None, None, :]
    idx = (n2 * (k1 + N1 * k2)) % SEQ
    th = 2 * np.pi * idx / SEQ
    sc = 1.0 / math.sqrt(SEQ)
    m = np.concatenate([np.cos(th) * sc, -np.sin(th) * sc], axis=2).astype(np.float32).astype(bf)
    m = np.transpose(m, (1, 0, 2))
    return np.ascontiguousarray(d1), np.ascontiguousarray(m)


def _shared_inputs(rel_bias, attn_norm, attn_w_in, attn_w_out, attn_sink, fourier_norm, fourier_w_gate,
                   fourier_w_out, final_norm, NBP, NBS):
    bf = ml_dtypes.bfloat16
    w = np.asarray(attn_w_in[0], np.float32)
    hp = []
    for kc in range(2):
        for i in range(4):
            hp += [8 * kc + i, 8 * kc + 4 + i]
    cols_q = np.concatenate([np.arange(64 * h, 64 * h + 64) for h in hp])
    q = w[:, cols_q]
    gg = w[:, 1536 + cols_q]
    kk = w[:, 1024:1280]
    vv = w[:, 1280:1536]
    w_in = np.ascontiguousarray(np.concatenate([q, kk, gg, vv], axis=1))
    w_out_p = np.ascontiguousarray(np.asarray(attn_w_out[0], np.float32)[cols_q, :])
    gvec = np.zeros((128, 16), np.float32)
    gvec[:, 0:8] = np.asarray(attn_norm[0], np.float32).reshape(8, 128).T
    gvec[:, 8:16] = np.asarray(fourier_norm[0], np.float32).reshape(8, 128).T
    gf = np.ascontiguousarray(np.broadcast_to(np.asarray(final_norm, np.float32)[None, :], (128, D)))
    rb = np.concatenate([np.asarray(rel_bias, np.float32), np.full((1, 16), NEG, np.float32)], axis=0)
    key = np.arange(128)[:, None]
    qi = np.arange(128)[None, :]
    biasT = np.zeros((128, 12, 4, 128), np.float32)
    for j in range(3):
        rel = 128 * (j - 1) + key - qi
        bucket = _t5_bucket(rel)
        bidx = np.where(np.abs(rel) <= 128, bucket, 32)
        for g in range(4):
            for hh in range(4):
                biasT[:, j * 4 + g, hh, :] = rb[bidx, 4 * g + hh]
    biasT = np.ascontiguousarray(biasT.reshape(128, 12 * 512))
    sink = np.asarray(attn_sink[0], np.float32)
    sinkrep = np.zeros((1, 4, 4, 128), np.float32)
    for g in range(4):
        for hh in range(4):
            sinkrep[0, g, hh, :] = sink[4 * g + hh]
    sinkrep = np.ascontiguousarray(sinkrep.reshape(1, 2048))
    sinkcol = np.ascontiguousarray(np.broadcast_to(sink[None, :], (128, 16))).astype(np.float32)
    onesz = np.zeros((128, 2, 128), np.float32)
    onesz[:, 0, 0:64] = 1.0
    onesz[:, 1, 64:128] = 1.0
    onesz = np.ascontiguousarray(onesz.reshape(128, 256)).astype(bf)
    r = (128 * np.arange(2)[None, :, None] + np.arange(128)[:, None, None])
    cc = np.arange(256)[None, None, :]
    ang = 2 * np.pi * ((r * cc) % 256) / 256
    cs0 = np.concatenate([np.cos(ang) / 16.0, np.sin(ang) / 16.0], axis=2).astype(np.float32)
    cs0 = np.ascontiguousarray(cs0.reshape(128, 1024))
    d1p, mp = _consts(NBP)
    d1s, ms = _consts(NBS)
    return dict(
        w_in=w_in, w_out=w_out_p, onesz=onesz,
        wg=np.ascontiguousarray(np.asarray(fourier_w_gate[0], np.float32)),
        w_out2=np.ascontiguousarray(np.asarray(fourier_w_out[0], np.float32)),
        gvec=gvec, gf=gf, biasT=biasT, sinkrep=sinkrep, sinkcol=sinkcol, cs0=cs0,
        ident=np.eye(128, dtype=np.float32).astype(bf), ones=np.ones((128, 128), np.float32).astype(bf),
        d1p=d1p, d1s=d1s, mp=mp, ms=ms,
    )


def run(xps, xss, weights, NBP, NBS, ncores):
    nc = build(NBP, NBS)
    shared = _shared_inputs(NBP=NBP, NBS=NBS, **weights)
    in_maps = []
    for c in range(ncores):
        m = dict(shared)
        m["xp"] = np.ascontiguousarray(xps[c], dtype=np.float32)
        m["xs"] = np.ascontiguousarray(xss[c], dtype=np.float32)
        in_maps.append(m)
    res = run_bass_kernel_spmd(nc, in_maps, core_ids=list(range(ncores)), **({'trace': True} if os.environ.get('KTRACE') else {}))
    if os.environ.get('KTRACE'):
        print('EXEC_NS', res.exec_time_ns)
    return [r["yp"] for r in res.results], [r["ys"] for r in res.results]


def kernel(x_prompt, x_sample, rel_bias, attn_norm, attn_w_in, attn_w_out, attn_sink,
           fourier_norm, fourier_w_gate, fourier_w_out, final_norm):
    x_prompt = np.asarray(x_prompt, np.float32)
    x_sample = np.asarray(x_sample, np.float32)
    NBP, NBS = 128, 32
    weights = dict(rel_bias=np.asarray(rel_bias), attn_norm=np.asarray(attn_norm), attn_w_in=np.asarray(attn_w_in),
                   attn_w_out=np.asarray(attn_w_out), attn_sink=np.asarray(attn_sink),
                   fourier_norm=np.asarray(fourier_norm), fourier_w_gate=np.asarray(fourier_w_gate),
                   fourier_w_out=np.asarray(fourier_w_out), final_norm=np.asarray(final_norm))
    zp = np.zeros((NBP * 128, D), np.float32)
    zs = np.zeros((NBS * 128, D), np.float32)
    xps = [x_prompt[c] if c < 2 else zp for c in range(8)]
    xss = [x_sample[c] if c < 4 else zs for c in range(8)]
    yps, yss = run(xps, xss, weights, NBP, NBS, 8)
    y_prompt = np.stack([yps[0], yps[1]], axis=0).astype(np.float32)
    y_sample = np.stack([yss[c] for c in range(4)], axis=0).astype(np.float32)
    return (y_prompt, y_sample)
```
